# Optimizing a Trainium2 kernel written in Bass

```python
import jax
import jax.numpy as jnp
from jax import lax
import numpy as np

D_MODEL = 1024
BATCH = 8
SEQ = 4096
DEPTH = 4

GRID_W = 64
CTX_LEN = 256
N_MIXERS = 2
N_GLA_LAYERS = (DEPTH + N_MIXERS - 1) // N_MIXERS
N_CONV_LAYERS = DEPTH // N_MIXERS
GLA_HEADS = 4
GLA_DK = D_MODEL // 2
GLA_DV = D_MODEL
GLA_HEAD_K = GLA_DK // GLA_HEADS
GLA_HEAD_V = GLA_DV // GLA_HEADS
GLA_GATE_RANK = 16
GLA_GATE_TAU = 16.0
GLA_CHUNK = 64
GLA_IN = 2 * GLA_DK + 2 * GLA_DV + 2 * GLA_GATE_RANK
CONV_WIDTH = 31
FFN_HIDDEN = -(-8 * D_MODEL // (3 * 256)) * 256
NORM_EPS = 1e-6

kernel_name = 'hybrid_gla_conformer_prefix_dit'


def rmsnorm(x, g):
    xf = x.astype(jnp.float32)
    y = xf * lax.rsqrt(jnp.mean(xf * xf, axis=-1, keepdims=True) + NORM_EPS)
    return (y * g.astype(jnp.float32)).astype(x.dtype)


def layernorm(x, g, b):
    xf = x.astype(jnp.float32)
    mu = jnp.mean(xf, axis=-1, keepdims=True)
    xc = xf - mu
    y = xc * lax.rsqrt(jnp.mean(xc * xc, axis=-1, keepdims=True) + NORM_EPS)
    return (y * g.astype(jnp.float32) + b.astype(jnp.float32)).astype(x.dtype)


def modulate(h, shift, scale):
    return h * (1.0 + scale) + shift


def to_grid_order(t, col_major):
    if not col_major:
        return t
    b, l, d = t.shape
    rows = l // GRID_W
    return t.reshape(b, rows, GRID_W, d).transpose(0, 2, 1, 3).reshape(b, l, d)


def from_grid_order(t, col_major):
    if not col_major:
        return t
    b, l, d = t.shape
    rows = l // GRID_W
    return t.reshape(b, GRID_W, rows, d).transpose(0, 2, 1, 3).reshape(b, l, d)


def split_heads(t, head_dim):
    b, l, _ = t.shape
    return t.reshape(b, l, GLA_HEADS, head_dim).transpose(0, 2, 1, 3)


def gla_project(h, w_in, wa_f, ba_f, wa_b, ba_b):
    z = (h @ w_in).astype(jnp.float32)
    o1 = GLA_DK
    o2 = 2 * GLA_DK
    o3 = o2 + GLA_DV
    o4 = o3 + GLA_DV
    o5 = o4 + GLA_GATE_RANK
    q = split_heads(z[..., :o1], GLA_HEAD_K) * (GLA_HEAD_K ** -0.5)
    k = split_heads(z[..., o1:o2], GLA_HEAD_K)
    v = split_heads(z[..., o2:o3], GLA_HEAD_V)
    r = z[..., o3:o4]
    lg_f = jax.nn.log_sigmoid(z[..., o4:o5] @ wa_f.astype(jnp.float32) + ba_f.astype(jnp.float32)) / GLA_GATE_TAU
    lg_b = jax.nn.log_sigmoid(z[..., o5:] @ wa_b.astype(jnp.float32) + ba_b.astype(jnp.float32)) / GLA_GATE_TAU
    return q, k, v, r, split_heads(lg_f, GLA_HEAD_K), split_heads(lg_b, GLA_HEAD_K)


def gla_chunked(q, k, v, log_g, s0, strict):
    b_, h_, l_, _ = q.shape
    dv = v.shape[-1]
    n = l_ // GLA_CHUNK
    ch = lambda t: t.reshape(b_, h_, n, GLA_CHUNK, t.shape[-1])
    q, k, v, log_g = ch(q), ch(k), ch(v), ch(log_g)
    bcum = jnp.cumsum(log_g, axis=3)
    b_last = bcum[:, :, :, -1:, :]
    qe = q * jnp.exp(bcum)
    ke = k * jnp.exp(-bcum)
    kd = k * jnp.exp(b_last - bcum)
    mask = jnp.tril(jnp.ones((GLA_CHUNK, GLA_CHUNK), dtype=bool), k=-1 if strict else 0)
    a = jnp.where(mask, jnp.einsum('bhncd,bhnsd->bhncs', qe, ke), 0.0)
    o_intra = jnp.einsum('bhncs,bhnse->bhnce', a, v)
    upd = jnp.einsum('bhncd,bhnce->nbhde', kd, v)
    decay = jnp.exp(b_last[:, :, :, 0, :]).transpose(2, 0, 1, 3)

    def step(s, inp):
        dcy, u = inp
        return dcy[..., None] * s + u, s

    s_final, s_in = lax.scan(step, s0, (decay, upd))
    o_inter = jnp.einsum('bhncd,nbhde->bhnce', qe, s_in)
    return (o_intra + o_inter).reshape(b_, h_, l_, dv), s_final


def gla_bidirectional(q, k, v, lg_f, lg_b, s0_f, s0_b):
    o_f, s_f = gla_chunked(q, k, v, lg_f, s0_f, strict=False)
    flip = lambda t: jnp.flip(t, axis=2)
    o_b, s_b = gla_chunked(flip(q), flip(k), flip(v), flip(lg_b), s0_b, strict=True)
    return o_f + flip(o_b), s_f, s_b


def gla_output(o, r, norm_g, w_out, dtype):
    b_, h_, l_, hv = o.shape
    o = o.transpose(0, 2, 1, 3)
    o = o * lax.rsqrt(jnp.mean(o * o, axis=-1, keepdims=True) + NORM_EPS)
    o = o * norm_g.astype(jnp.float32).reshape(GLA_HEADS, GLA_HEAD_V)
    o = o.reshape(b_, l_, h_ * hv) * jax.nn.silu(r)
    return o.astype(dtype) @ w_out


def conv_module(h, w_pw1, b_pw1, w_dw, b_dw, ln_g, ln_b, w_pw2, b_pw2):
    u = h @ w_pw1 + b_pw1
    u = u[..., :D_MODEL] * jax.nn.sigmoid(u[..., D_MODEL:])
    u = lax.conv_general_dilated(
        u, w_dw[:, None, :].astype(u.dtype), window_strides=(1,),
        padding=[(CONV_WIDTH // 2, CONV_WIDTH // 2)],
        dimension_numbers=('NWC', 'WIO', 'NWC'), feature_group_count=D_MODEL) + b_dw
    u = jax.nn.silu(layernorm(u, ln_g, ln_b))
    return u @ w_pw2 + b_pw2


def swiglu(h, w_in, w_out):
    u = h @ w_in
    return (jax.nn.silu(u[..., :FFN_HIDDEN]) * u[..., FFN_HIDDEN:]) @ w_out


def setup_inputs(seed: int = 0) -> dict:
    key = jax.random.key(seed)
    ks = jax.random.split(key, 32)
    nrm = lambda k, shape, scale: jax.random.normal(k, shape, jnp.float32) * scale
    D = D_MODEL
    G = N_GLA_LAYERS
    C = N_CONV_LAYERS
    return {
        'x': nrm(ks[0], (BATCH, SEQ, D), 1.0),
        'c': nrm(ks[1], (BATCH, D), 1.0),
        'ctx': nrm(ks[2], (BATCH, CTX_LEN, D), 1.0),
        'c_ctx': nrm(ks[3], (D,), 1.0),
        'w_mod': nrm(ks[4], (DEPTH, D, 6 * D), 0.5 * D ** -0.5),
        'b_mod': nrm(ks[5], (DEPTH, 6 * D), 0.02),
        'norm_mix_g': 1.0 + nrm(ks[6], (DEPTH, D), 0.02),
        'norm_ffn_g': 1.0 + nrm(ks[7], (DEPTH, D), 0.02),
        'gla_w_in': nrm(ks[8], (G, D, GLA_IN), D ** -0.5),
        'gla_wa_f': nrm(ks[9], (G, GLA_GATE_RANK, GLA_DK), GLA_GATE_RANK ** -0.5),
        'gla_ba_f': nrm(ks[10], (G, GLA_DK), 0.1),
        'gla_wa_b': nrm(ks[11], (G, GLA_GATE_RANK, GLA_DK), GLA_GATE_RANK ** -0.5),
        'gla_ba_b': nrm(ks[12], (G, GLA_DK), 0.1),
        'gla_norm_g': 1.0 + nrm(ks[13], (G, GLA_DV), 0.02),
        'gla_w_out': nrm(ks[14], (G, GLA_DV, D), GLA_DV ** -0.5),
        'conv_w_pw1': nrm(ks[15], (C, D, 2 * D), D ** -0.5),
        'conv_b_pw1': nrm(ks[16], (C, 2 * D), 0.02),
        'conv_w_dw': nrm(ks[17], (C, CONV_WIDTH, D), CONV_WIDTH ** -0.5),
        'conv_b_dw': nrm(ks[18], (C, D), 0.02),
        'conv_ln_g': 1.0 + nrm(ks[19], (C, D), 0.02),
        'conv_ln_b': nrm(ks[20], (C, D), 0.02),
        'conv_w_pw2': nrm(ks[21], (C, D, D), D ** -0.5),
        'conv_b_pw2': nrm(ks[22], (C, D), 0.02),
        'ffn_w_in': nrm(ks[23], (DEPTH, D, 2 * FFN_HIDDEN), D ** -0.5),
        'ffn_w_out': nrm(ks[24], (DEPTH, FFN_HIDDEN, D), FFN_HIDDEN ** -0.5),
        'final_norm_g': 1.0 + nrm(ks[25], (D,), 0.02),
    }


def reference(x, c, ctx, c_ctx, w_mod, b_mod, norm_mix_g, norm_ffn_g,
              gla_w_in, gla_wa_f, gla_ba_f, gla_wa_b, gla_ba_b, gla_norm_g, gla_w_out,
              conv_w_pw1, conv_b_pw1, conv_w_dw, conv_b_dw, conv_ln_g, conv_ln_b, conv_w_pw2, conv_b_pw2,
              ffn_w_in, ffn_w_out, final_norm_g):
    batch = x.shape[0]
    for i in range(DEPTH):
        last = i == DEPTH - 1
        j = i // N_MIXERS
        col_major = j % 2 == 1
        mod_lat = jax.nn.silu(c) @ w_mod[i] + b_mod[i]
        mod_ctx = jax.nn.silu(c_ctx) @ w_mod[i] + b_mod[i]
        sh1, sc1, gt1, sh2, sc2, gt2 = jnp.split(mod_lat[:, None, :], 6, axis=-1)
        csh1, csc1, cgt1, csh2, csc2, cgt2 = jnp.split(mod_ctx, 6, axis=-1)
        h_lat = to_grid_order(modulate(rmsnorm(x, norm_mix_g[i]), sh1, sc1), col_major)
        y_ctx = None
        if i % N_MIXERS == 0:
            p = (gla_w_in[j], gla_wa_f[j], gla_ba_f[j], gla_wa_b[j], gla_ba_b[j])
            h_ctx = modulate(rmsnorm(ctx, norm_mix_g[i]), csh1, csc1)
            qc, kc, vc, rc, gfc, gbc = gla_project(h_ctx, *p)
            ql, kl, vl, rl, gfl, gbl = gla_project(h_lat, *p)
            zero = jnp.zeros((batch, GLA_HEADS, GLA_HEAD_K, GLA_HEAD_V), jnp.float32)
            o_ctx, s_f, s_b = gla_bidirectional(qc, kc, vc, gfc, gbc, zero, zero)
            o_lat, _, _ = gla_bidirectional(ql, kl, vl, gfl, gbl, s_f, s_b)
            y_lat = gla_output(o_lat, rl, gla_norm_g[j], gla_w_out[j], x.dtype)
            if not last:
                y_ctx = gla_output(o_ctx, rc, gla_norm_g[j], gla_w_out[j], ctx.dtype)
        else:
            p = (conv_w_pw1[j], conv_b_pw1[j], conv_w_dw[j], conv_b_dw[j],
                 conv_ln_g[j], conv_ln_b[j], conv_w_pw2[j], conv_b_pw2[j])
            y_lat = conv_module(h_lat, *p)
            if not last:
                h_ctx = modulate(rmsnorm(ctx, norm_mix_g[i]), csh1, csc1)
                y_ctx = conv_module(h_ctx, *p)
        x = x + gt1 * from_grid_order(y_lat, col_major)
        x = x + gt2 * swiglu(modulate(rmsnorm(x, norm_ffn_g[i]), sh2, sc2), ffn_w_in[i], ffn_w_out[i])
        if not last:
            ctx = ctx + cgt1 * y_ctx
            ctx = ctx + cgt2 * swiglu(modulate(rmsnorm(ctx, norm_ffn_g[i]), csh2, csc2), ffn_w_in[i], ffn_w_out[i])
    return rmsnorm(x, final_norm_g)
```

```python
import numpy as np
from contextlib import ExitStack
import concourse.bass as bass
import concourse.mybir as mybir
from concourse.bass_utils import run_bass_kernel_spmd

F32 = mybir.dt.float32
BF16 = mybir.dt.bfloat16
AF = mybir.ActivationFunctionType
ALU = mybir.AluOpType
AX = mybir.AxisListType

D = 1024
L = 4096
CT = 256
NTOK = L + CT
H = 4
DK = 128
DV = 256
FH = 2816
GIN = 3104
DEPTH = 4
CW = 31
EPS = 1e-6
KC = 8


class Tok:
    __slots__ = ("sem", "val")

    def __init__(self, sem, val):
        self.sem = sem
        self.val = val


class Buf:
    def __init__(self, name="b"):
        self.name = name
        self.last_w = None
        self.readers = []


class Eng:
    LIMIT = 30000

    def __init__(self, K, name, eng):
        self.K = K
        self.name = name
        self.eng = eng
        self.sem = None
        self.count = 0
        self.nsem = 0
        self.waited = {}
        self.pend_r = []
        self.pend_w = []

    def _newsem(self):
        self.sem = self.K.es.enter_context(self.K.nc.semaphore(f"s_{self.name}_{self.nsem}"))
        self.nsem += 1
        self.count = 0

    def wait(self, tok):
        if tok is None:
            return
        key = id(tok.sem)
        if self.waited.get(key, 0) >= tok.val:
            return
        self.eng.wait_ge(tok.sem, tok.val)
        self.waited[key] = tok.val


class Kern:
    NDMA = 28

    def __init__(self, nc):
        self.nc = nc
        self.es = ExitStack()
        self.pe = Eng(self, "pe", nc.tensor)
        self.act = Eng(self, "act", nc.scalar)
        self.dve = Eng(self, "dve", nc.vector)
        self.pool = Eng(self, "pool", nc.gpsimd)
        self.sp = Eng(self, "sp", nc.sync)
        self.all = (self.pe, self.act, self.dve, self.pool, self.sp)
        self.dma_pools = {}
        self.phase_toks = []
        self.ninst = 0

    def _deps(self, E, reads, writes):
        for b in reads:
            for e in self.all:
                if e is not E and b in e.pend_w:
                    raise RuntimeError(f"pending writer on {b.name}")
            E.wait(b.last_w)
        for b in writes:
            for e in self.all:
                if e is not E and (b in e.pend_w or b in e.pend_r):
                    raise RuntimeError(f"pending access on {b.name}")
            E.wait(b.last_w)
            for r in b.readers:
                E.wait(r)

    def _commit(self, tok, reads, writes):
        for b in writes:
            b.last_w = tok
            b.readers = []
        for b in reads:
            if b not in writes:
                b.readers.append(tok)

    def op(self, E, fn, reads=(), writes=(), inc=True):
        reads = list(reads)
        writes = list(writes)
        self._deps(E, reads, writes)
        ins = fn(E.eng)
        self.ninst += 1
        if inc:
            if E.sem is None or E.count >= Eng.LIMIT:
                E._newsem()
            E.count += 1
            ins.then_inc(E.sem, 1)
            tok = Tok(E.sem, E.count)
            self._commit(tok, reads + E.pend_r, writes + E.pend_w)
            E.pend_r = []
            E.pend_w = []
            return tok
        E.pend_r += reads
        E.pend_w += writes
        return None

    def dma(self, E, out, in_, reads=(), writes=(), track=True, sempool=None):
        reads = list(reads)
        writes = list(writes)
        self._deps(E, reads, writes)
        pool_ = self.dma_pools.setdefault(sempool or E.name, [[], 0])
        nmax = self.NDMA if E is self.sp else 12
        if len(pool_[0]) < nmax:
            sem = self.es.enter_context(self.nc.semaphore(f"s_dma_{sempool or E.name}_{len(pool_[0])}"))
            slot = [sem, 0]
            pool_[0].append(slot)
        else:
            slot = pool_[0][pool_[1] % nmax]
            pool_[1] += 1
            E.wait(Tok(slot[0], slot[1]))
            if slot[1] >= 30000:
                slot[0] = self.es.enter_context(self.nc.semaphore(f"s_dmax_{E.name}_{pool_[1]}"))
                slot[1] = 0
        ins = E.eng.dma_start(out=out, in_=in_)
        self.ninst += 1
        slot[1] += 16
        ins.then_inc(slot[0], 16)
        tok = Tok(slot[0], slot[1])
        self._commit(tok, reads, writes)
        if track:
            self.phase_toks.append(tok)
        return tok

    def barrier(self):
        for t in self.phase_toks:
            self.sp.wait(t)
        self.phase_toks = []
        for e in (self.pe, self.act, self.dve, self.pool):
            assert not e.pend_r and not e.pend_w
            if e.sem is not None:
                self.sp.wait(Tok(e.sem, e.count))
        tok = self.op(self.sp, lambda e: e.nop())
        for e in (self.pe, self.act, self.dve, self.pool):
            e.wait(tok)


class T:
    def __init__(self, t, name):
        self.t = t
        self.b = Buf(name)

    def __getitem__(self, idx):
        return self.t[idx]


def build(nlayers=DEPTH, dbg=None):
    nc = bass.Bass("TRN2", target_bir_lowering=False)
    K = Kern(nc)
    es = K.es

    def din(name, shape):
        return nc.dram_tensor(name, list(shape), F32, kind="ExternalInput").ap()

    x_in = din("x", [L, D])
    ctx_in = din("ctx", [CT, D])
    c_in = din("c", [1, D])
    cctx_in = din("c_ctx", [1, D])
    w_mod = din("w_mod", [DEPTH, D, 6 * D])
    b_mod = din("b_mod", [DEPTH, 6 * D])
    norm_mix_g = din("norm_mix_g", [DEPTH, D])
    norm_ffn_g = din("norm_ffn_g", [DEPTH, D])
    gla_w_in = din("gla_w_in", [2, D, GIN])
    gla_wa_f = din("gla_wa_f", [2, 16, 512])
    gla_ba_f = din("gla_ba_f", [2, 512])
    gla_wa_b = din("gla_wa_b", [2, 16, 512])
    gla_ba_b = din("gla_ba_b", [2, 512])
    gla_norm_g = din("gla_norm_g", [2, D])
    gla_w_out = din("gla_w_out", [2, D, D])
    conv_w_pw1 = din("conv_w_pw1", [2, D, 2 * D])
    conv_b_pw1 = din("conv_b_pw1", [2, 2 * D])
    conv_w_dw = din("conv_w_dw", [2, CW, D])
    conv_b_dw = din("conv_b_dw", [2, D])
    conv_ln_g = din("conv_ln_g", [2, D])
    conv_ln_b = din("conv_ln_b", [2, D])
    conv_w_pw2 = din("conv_w_pw2", [2, D, D])
    conv_b_pw2 = din("conv_b_pw2", [2, D])
    ffn_w_in = din("ffn_w_in", [DEPTH, D, 2 * FH])
    ffn_w_out = din("ffn_w_out", [DEPTH, FH, D])
    final_norm_g = din("final_norm_g", [1, D])
    out = nc.dram_tensor("out", [L, D], F32, kind="ExternalOutput").ap()

    def dscr(name, shape, dt=F32):
        return nc.dram_tensor(name, list(shape), dt, kind="Internal").ap()

    XD = dscr("XD", [NTOK, D])
    gla_w_in_b = dscr("gla_w_in_b", [2, D, GIN], BF16)
    gla_w_out_b = dscr("gla_w_out_b", [2, D, D], BF16)
    pw1_b = dscr("pw1_b", [2, D, 2 * D], BF16)
    pw2_b = dscr("pw2_b", [2, D, D], BF16)
    ffn_in_b = dscr("ffn_in_b", [DEPTH, 6, 128, KC * 2 * 512], BF16)
    ffn_out_b = dscr("ffn_out_b", [DEPTH, FH, D], BF16)
    SP_TM = dscr("SP_TM", [NTOK, 3072])
    SP_V = dscr("SP_V", [NTOK, D], BF16)
    SP_FM = dscr("SP_FM", [8, 128, NTOK])
    UD = dscr("UD", [KC, 128, NTOK], BF16)

    nctr = [0]

    def sb(name, shape, dt=F32, stack=None):
        nctr[0] += 1
        name = f"{name}_{nctr[0]}"
        return T((stack or es).enter_context(nc.sbuf_tensor(name, list(shape), dt)), name)

    PE, ACT, DVE, POOL, SP = K.pe, K.act, K.dve, K.pool, K.sp

    banks = [T(es.enter_context(nc.psum_tensor(f"bank{i}", [128, 512], F32)), f"bank{i}") for i in range(8)]
    bank_rr = [0]
    bank_pool = [list(range(8))]
    obank_rr = [0]

    def bank():
        p = bank_pool[0]
        b = banks[p[bank_rr[0] % len(p)]]
        bank_rr[0] += 1
        return b

    def obank_pair():
        q = obank_rr[0] % 2
        obank_rr[0] += 1
        return [banks[2 * q], banks[2 * q + 1]]

    WB = {}

    def cast(name, dst, src):
        WB[name] = Buf(name)
        K.dma(POOL, dst, src, writes=[WB[name]], track=False)

    pending_casts = []

    def queue_cast(name, dst, src, nchunk):
        rows = src.shape[0]
        assert rows % nchunk == 0
        step = rows // nchunk
        WB[name] = [Buf(f"{name}_{q}") for q in range(nchunk)]
        for q in range(nchunk):
            pending_casts.append((WB[name][q], dst[q * step:(q + 1) * step, :], src[q * step:(q + 1) * step, :]))

    def bg_cast(n=1):
        for _ in range(n):
            if not pending_casts:
                return
            b_, dst, src = pending_casts.pop(0)
            K.dma(POOL, dst, src, writes=[b_], track=False, sempool="cast")

    def casts_for_layer(i):
        j = i // 2
        if i % 2 == 0:
            queue_cast(f"gin{j}", gla_w_in_b[j], gla_w_in[j], 4)
            queue_cast(f"gout{j}", gla_w_out_b[j], gla_w_out[j], 2)
        else:
            queue_cast(f"pw1{j}", pw1_b[j], conv_w_pw1[j], 4)
            queue_cast(f"pw2{j}", pw2_b[j], conv_w_pw2[j], 2)
        WB[f"fin{i}"] = []
        wsrc = ffn_w_in[i].rearrange("(k p) n -> p k n", p=128)
        for pc in range(6):
            w = 512 if pc < 5 else 256
            dstv = ffn_in_b[i, pc].rearrange("p (k h w) -> p k h w", k=KC, h=2)
            for hh in range(2):
                b_ = Buf(f"fin{i}_{pc}_{hh}")
                WB[f"fin{i}"].append(b_)
                pending_casts.append((b_, dstv[:, :, hh, 0:w], wsrc[:, :, hh * FH + pc * 512: hh * FH + pc * 512 + w]))
        queue_cast(f"fout{i}", ffn_out_b[i], ffn_w_out[i], 4)

    casts_for_layer(0)
    bg_cast(4)

    ident = sb("ident", [128, 128])
    ltf = sb("ltf", [128, 128])
    ltb = sb("ltb", [128, 128])
    mskf = sb("mskf", [128, H, 128])
    mskb = sb("mskb", [128, H, 128])
    sel2 = sb("sel2", [2, 2, 128])
    ones_bf = sb("ones_bf", [128, 128], BF16)
    ones_f = sb("ones_f", [128, 128])

    def aff(dst, pattern, cm, cmp, base=0):
        K.op(POOL, lambda e: e.affine_select(out=dst, in_=dst, pattern=pattern, compare_op=cmp, fill=0.0,
                                             base=base, channel_multiplier=cm), writes=[CONST])

    CONST = Buf("const")
    K.op(POOL, lambda e: e.memset(ident[:], 1.0), writes=[CONST])
    aff(ident[:], [[-1, 128]], 1, ALU.is_equal)
    K.op(POOL, lambda e: e.memset(ltf[:], 1.0), writes=[CONST])
    aff(ltf[:], [[1, 128]], -1, ALU.is_ge)
    K.op(POOL, lambda e: e.memset(ltf[0:64, 64:128], 0.0), writes=[CONST])
    K.op(POOL, lambda e: e.memset(ltb[:], 1.0), writes=[CONST])
    aff(ltb[:], [[-1, 128]], 1, ALU.is_ge)
    K.op(POOL, lambda e: e.memset(ltb[64:128, 0:64], 0.0), writes=[CONST])
    for h in range(H):
        K.op(POOL, lambda e: e.tensor_copy(mskf[:, h, :], ltf[:]), writes=[CONST])
        K.op(POOL, lambda e: e.tensor_copy(mskb[:, h, :], ltb[:]), writes=[CONST])
        aff(mskb[:, h, :], [[-1, 128]], 1, ALU.is_gt)
    K.op(POOL, lambda e: e.memset(sel2[:], 1.0), writes=[CONST])
    aff(sel2[:, 0, :], [[0, 128]], 1, ALU.is_equal)
    aff(sel2[:, 1, :], [[0, 128]], 1, ALU.is_equal, base=-1)
    K.op(POOL, lambda e: e.memset(ones_bf[:], 1.0), writes=[CONST])
    K.op(POOL, lambda e: e.memset(ones_f[:], 1.0), writes=[CONST])

    colv = sb("colv", [128, 4, KC, 2])
    gtbc = [[sb(f"gtbc{m}{g}", [128, D]) for g in range(2)] for m in range(2)]
    scT = sb("scT", [128, KC, 2])
    gmix = sb("gmix", [128, DEPTH, KC])
    gffn = sb("gffn", [128, DEPTH, KC])

    def rows_to_cols(dst_t, dst_ap, row_ap_fn, n, reads):
        bk = bank()
        for i in range(n):
            K.op(PE, lambda e: e.matmul(bk[:, i:i + 1], row_ap_fn(i), ident[0:1, 0:1], start=True, stop=True),
                 reads=reads + [CONST], writes=[bk.b], inc=(i == n - 1))
        K.op(DVE, lambda e: e.tensor_copy(dst_ap, bk[:, 0:n]), reads=[bk.b], writes=[dst_t.b])

    with ExitStack() as st:
        crow = sb("crow", [1, 2, D], stack=st)
        grow = sb("grow", [1, 2 * DEPTH, D], stack=st)
        K.dma(SP, crow[0:1, 0, :], c_in[0:1, :], writes=[crow.b])
        K.dma(SP, crow[0:1, 1, :], cctx_in[0:1, :], writes=[crow.b])
        K.dma(SP, grow[0:1, 0:DEPTH, :], norm_mix_g[None, :, :], writes=[grow.b])
        K.dma(SP, grow[0:1, DEPTH:2 * DEPTH, :], norm_ffn_g[None, :, :], writes=[grow.b])
        ctmp = sb("ctmp", [128, KC, 2], stack=st)
        bk = bank()
        for k in range(KC):
            for m in range(2):
                K.op(PE, lambda e: e.matmul(bk[:, 2 * k + m:2 * k + m + 1], crow[0:1, m, k * 128:(k + 1) * 128],
                                            ident[0:1, 0:1], start=True, stop=True),
                     reads=[crow.b, CONST], writes=[bk.b], inc=(k == KC - 1 and m == 1))
        K.op(ACT, lambda e: e.activation(out=scT[:].rearrange("p k m -> p (k m)"), in_=bk[:, 0:2 * KC], func=AF.Silu),
             reads=[bk.b], writes=[scT.b])
        for (dst, base) in ((gmix, 0), (gffn, DEPTH)):
            bk = bank()
            for l in range(DEPTH):
                for k in range(KC):
                    K.op(PE, lambda e: e.matmul(bk[:, l * KC + k:l * KC + k + 1], grow[0:1, base + l, k * 128:(k + 1) * 128],
                                                ident[0:1, 0:1], start=True, stop=True),
                         reads=[grow.b, CONST], writes=[bk.b], inc=(l == DEPTH - 1 and k == KC - 1))
            K.op(DVE, lambda e: e.tensor_copy(dst[:].rearrange("p l k -> p (l k)"), bk[:, 0:DEPTH * KC]),
                 reads=[bk.b], writes=[dst.b])
        K.barrier()

    colb_all = [sb(f"colb{q}", [128, 5, 2 * KC]) for q in range(2)]
    wdw_all = [sb(f"wdw{q}", [128, KC, CW]) for q in range(2)]
    for jj in range(nlayers // 2):
        with ExitStack() as st1:
            prow = sb("c_prow", [1, 5, 2 * D], stack=st1)
            K.dma(SP, prow[0:1, 0, :], conv_b_pw1[jj:jj + 1, :], writes=[prow.b])
            K.dma(SP, prow[0:1, 1, 0:D], conv_b_dw[jj:jj + 1, :], writes=[prow.b])
            K.dma(SP, prow[0:1, 2, 0:D], conv_ln_g[jj:jj + 1, :], writes=[prow.b])
            K.dma(SP, prow[0:1, 3, 0:D], conv_ln_b[jj:jj + 1, :], writes=[prow.b])
            for v in range(4):
                nn = 16 if v == 0 else 8
                rows_to_cols(colb_all[jj], colb_all[jj][:, v, 0:nn], lambda q, v=v: prow[0:1, v, q * 128:(q + 1) * 128], nn, [prow.b])
            wrow = sb("c_wrow", [1, CW, D], stack=st1)
            K.dma(SP, wrow[0:1, :, :], conv_w_dw[jj:jj + 1, :, :], writes=[wrow.b])
            for c in range(KC):
                rows_to_cols(wdw_all[jj], wdw_all[jj][:, c, :], lambda q, c=c: wrow[0:1, q, c * 128:(c + 1) * 128], CW, [wrow.b])
            K.barrier()

    def mod_phase(i):
        with ExitStack() as st:
            mod2 = sb("mod2", [2, 6 * D], stack=st)
            bm2 = sb("bm2", [2, 6 * D], stack=st)
            wm = [sb(f"wm{q}", [128, KC, 512], stack=st) for q in range(2)]
            K.dma(SP, bm2[0:1, :], b_mod[i:i + 1, :], writes=[bm2.b])
            K.dma(SP, bm2[1:2, :], b_mod[i:i + 1, :], writes=[bm2.b])
            wv = w_mod[i].rearrange("(k p) n -> p k n", p=128)
            for cb in range(12):
                w = wm[cb % 2]
                K.dma(SP, w[:], wv[:, :, cb * 512:(cb + 1) * 512], writes=[w.b])
                bk = bank()
                for k in range(KC):
                    K.op(PE, lambda e: e.matmul(bk[0:2, :], scT[:, k, :], w[:, k, :], start=(k == 0), stop=(k == KC - 1)),
                         reads=[scT.b, w.b], writes=[bk.b], inc=(k == KC - 1))
                K.op(DVE, lambda e: e.tensor_tensor(out=mod2[:, cb * 512:(cb + 1) * 512], in0=bk[0:2, :],
                                                    in1=bm2[:, cb * 512:(cb + 1) * 512], op=ALU.add),
                     reads=[bk.b, bm2.b], writes=[mod2.b])
            bk = bank()
            segs = (0, 1, 3, 4)
            n = 0
            for vi, seg in enumerate(segs):
                for k in range(KC):
                    col = (vi * KC + k) * 2
                    K.op(PE, lambda e: e.matmul(bk[:, col:col + 2], mod2[0:2, seg * D + k * 128: seg * D + (k + 1) * 128],
                                                ident[0:2, 0:2], start=True, stop=True),
                         reads=[mod2.b, CONST], writes=[bk.b], inc=(vi == 3 and k == KC - 1))
            K.op(DVE, lambda e: e.tensor_copy(colv[:].rearrange("p v k m -> p (v k m)"), bk[:, 0:4 * KC * 2]),
                 reads=[bk.b], writes=[colv.b])
            for (vi, g) in ((1, gmix), (3, gffn)):
                for m in range(2):
                    K.op(DVE, lambda e: e.scalar_tensor_tensor(out=colv[:, vi, :, m], in0=colv[:, vi, :, m], scalar=1.0,
                                                               in1=g[:, i, :], op0=ALU.add, op1=ALU.mult),
                         reads=[colv.b, g.b], writes=[colv.b])
            for m in range(2):
                for gi, seg in enumerate((2, 5)):
                    for hf in range(2):
                        bk = bank()
                        K.op(PE, lambda e: e.matmul(bk[:, :], sel2[0:2, m, :], mod2[0:2, seg * D + hf * 512: seg * D + (hf + 1) * 512],
                                                    start=True, stop=True), reads=[mod2.b, CONST], writes=[bk.b])
                        K.op(ACT, lambda e: e.copy(gtbc[m][gi][:, hf * 512:(hf + 1) * 512], bk[:, :]),
                             reads=[bk.b], writes=[gtbc[m][gi].b])
            K.barrier()

    def tile_rows(src, st, col_major):
        if st < 2:
            return [(0, 128, src[st * 128:(st + 1) * 128, :])]
        t = st - 2
        if not col_major:
            return [(0, 128, src[t * 128:(t + 1) * 128, :])]
        v = src[0:L, :].rearrange("(r c) d -> c r d", c=64)
        return [(0, 64, v[2 * t]), (64, 64, v[2 * t + 1])]

    def xsrc(i, st):
        cm = (i // 2) % 2 == 1
        if i == 0:
            if st < 2:
                return [(0, 128, ctx_in[st * 128:(st + 1) * 128, :])]
            return tile_rows(x_in, st, cm)
        if st < 2:
            return [(0, 128, XD[L + st * 128: L + (st + 1) * 128, :])]
        return tile_rows(XD, st, cm)

    def xdst(i, st):
        cm = (i // 2) % 2 == 1
        if st < 2:
            return [(0, 128, XD[L + st * 128: L + (st + 1) * 128, :])]
        return tile_rows(XD, st, cm)

    def load_tile(dst_t, dst_ap_fn, pieces):
        for (p0, n, ap) in pieces:
            K.dma(SP, dst_ap_fn(p0, n), ap, writes=[dst_t.b])

    def store_tile(src_t, src_ap_fn, pieces):
        for (p0, n, ap) in pieces:
            K.dma(POOL, ap, src_ap_fn(p0, n), reads=[src_t.b])

    _SENT = object()

    def interleave(main, fill, k=1):
        for _ in main:
            for _q in range(k):
                if fill is None or next(fill, _SENT) is _SENT:
                    fill = None
                    break
        if fill is not None:
            for _ in fill:
                pass

    def drain(gen):
        for _ in gen:
            pass

    def prep_gen(xg, ntl, mi, which, hT, scr, bankfn=None):
        bankfn = bankfn or bank
        ss, lnv, rstd, junk, xn = scr
        K.op(DVE, lambda e: e.memset(ss[:, 0:ntl], 0.0), writes=[ss.b])
        for tl in range(ntl):
            K.op(ACT, lambda e: e.activation(out=junk[:], in_=xg[:, tl, :], func=AF.Square, accum_out=ss[:, tl:tl + 1]),
                 reads=[xg.b], writes=[junk.b, ss.b])
        K.op(ACT, lambda e: e.activation(out=lnv[:, 0:ntl], in_=ss[:, 0:ntl], func=AF.Ln, scale=1.0 / D, bias=EPS),
             reads=[ss.b], writes=[lnv.b])
        K.op(ACT, lambda e: e.activation(out=rstd[:, 0:ntl], in_=lnv[:, 0:ntl], func=AF.Exp, scale=-0.5),
             reads=[lnv.b], writes=[rstd.b])
        for tl in range(min(ntl, 2)):
            x_n = xn[tl % 2]
            K.op(DVE, lambda e: e.tensor_scalar(out=x_n[:], in0=xg[:, tl, :], scalar1=rstd[:, tl:tl + 1], scalar2=None, op0=ALU.mult),
                 reads=[xg.b, rstd.b], writes=[x_n.b])
        yield
        for tl in range(ntl):
            x_n = xn[tl % 2]
            if tl >= 2:
                K.op(DVE, lambda e: e.tensor_scalar(out=x_n[:], in0=xg[:, tl, :], scalar1=rstd[:, tl:tl + 1], scalar2=None, op0=ALU.mult),
                     reads=[xg.b, rstd.b], writes=[x_n.b])
            for hf in range(2):
                bk = bankfn()
                for kk in range(4):
                    k = hf * 4 + kk
                    K.op(PE, lambda e: e.transpose(bk[:, kk * 128:(kk + 1) * 128], x_n[:, k * 128:(k + 1) * 128], ident[:]),
                         reads=[x_n.b, CONST], writes=[bk.b], inc=(kk == 3))
                for kk in range(4):
                    k = hf * 4 + kk
                    eng = ACT if kk % 2 == 0 else DVE
                    if eng is ACT:
                        K.op(ACT, lambda e: e.activation(out=hT[:, k, tl * 128:(tl + 1) * 128], in_=bk[:, kk * 128:(kk + 1) * 128],
                                                         func=AF.Identity, scale=colv[:, 2 * which + 1, k, mi:mi + 1],
                                                         bias=colv[:, 2 * which, k, mi:mi + 1]),
                             reads=[bk.b, colv.b], writes=[hT.b])
                    else:
                        K.op(DVE, lambda e: e.tensor_scalar(out=hT[:, k, tl * 128:(tl + 1) * 128], in0=bk[:, kk * 128:(kk + 1) * 128],
                                                            scalar1=colv[:, 2 * which + 1, k, mi:mi + 1],
                                                            scalar2=colv[:, 2 * which, k, mi:mi + 1], op0=ALU.mult, op1=ALU.add),
                             reads=[bk.b, colv.b], writes=[hT.b])
                yield

    def prep_group(xg, ntl, mi, which, hT, scr):
        drain(prep_gen(xg, ntl, mi, which, hT, scr))

    def prep_scratch(st):
        ss = sb("p_ss", [128, 8], stack=st)
        lnv = sb("p_ln", [128, 8], stack=st)
        rstd = sb("p_rstd", [128, 8], stack=st)
        junk = sb("p_junk", [128, D], BF16, stack=st)
        xn = [sb(f"p_xn{q}", [128, D], stack=st) for q in range(2)]
        return (ss, lnv, rstd, junk, xn)

    def residual(xg, tl, hf, yb, gt, tmp):
        sl = slice(hf * 512, (hf + 1) * 512)
        K.op(DVE, lambda e: e.tensor_tensor(out=tmp[:], in0=yb[:, :], in1=gt[:, sl], op=ALU.mult),
             reads=[yb.b, gt.b], writes=[tmp.b])
        K.op(POOL, lambda e: e.tensor_tensor(out=xg[:, tl, sl], in0=xg[:, tl, sl], in1=tmp[:], op=ALU.add),
             reads=[tmp.b, xg.b], writes=[xg.b])

    def ffn_phase(i, last):
        bg_cast(100)
        if i + 1 < nlayers:
            casts_for_layer(i + 1)
        with ExitStack() as st:
            woutb = sb("woutb", [128, 22, D], BF16, stack=st)
            wo_v = ffn_out_b[i].rearrange("(k p) n -> p k n", p=128)
            for q in range(2):
                K.dma(SP, woutb[:, q * 11:(q + 1) * 11, :], wo_v[:, q * 11:(q + 1) * 11, :], reads=WB[f"fout{i}"], writes=[woutb.b])
            pieces = [sb(f"wpiece{q}", [128, KC, 2, 512], BF16, stack=st) for q in range(2)]
            xgs = [sb(f"f_xg{q}", [128, 4, D], stack=st) for q in range(2)]
            hTs = [sb(f"f_hT{q}", [128, KC, 512], BF16, stack=st) for q in range(2)]
            act = sb("f_act", [128, 22, 512], BF16, stack=st)
            sa = [sb(f"f_sa{q}", [128, 512], stack=st) for q in range(2)]
            tmp = [sb(f"f_tmp{q}", [128, 512], stack=st) for q in range(2)]
            scr = prep_scratch(st)
            if last:
                fg = sb("f_fg", [128, D], stack=st)
                bkg = [bank(), bank()]
                frow = sb("f_frow", [1, D], stack=st)
                K.dma(SP, frow[0:1, :], final_norm_g[0:1, :], writes=[frow.b])
                for hf in range(2):
                    K.op(PE, lambda e: e.matmul(bkg[hf][:, :], ones_f[0:1, :], frow[0:1, hf * 512:(hf + 1) * 512], start=True, stop=True),
                         reads=[frow.b, CONST], writes=[bkg[hf].b])
                    K.op(ACT, lambda e: e.copy(fg[:, hf * 512:(hf + 1) * 512], bkg[hf][:, :]), reads=[bkg[hf].b], writes=[fg.b])
                fss = sb("f_fss", [128, 4], stack=st)
                fln = sb("f_fln", [128, 4], stack=st)
                frs = sb("f_frs", [128, 4], stack=st)
                fo = [sb(f"f_fo{q}", [128, D], stack=st) for q in range(2)]
                fjunk = scr[3]
            groups = [(g * 512, 4, 0) for g in range(8)]
            if not last:
                groups.append((L, 2, 1))
            src = XD

            def load_group(gi):
                r0, ntl, mi = groups[gi]
                xg = xgs[gi % 2]
                for tl in range(ntl):
                    K.dma(SP, xg[:, tl, :], src[r0 + tl * 128: r0 + (tl + 1) * 128, :], writes=[xg.b])

            npc = 6

            def load_piece(pc, pt):
                K.dma(SP, pt[:].rearrange("p k h w -> p (k h w)"), ffn_in_b[i, pc],
                      reads=WB[f"fin{i}"][2 * pc: 2 * pc + 2], writes=[pt.b])

            load_group(0)
            pcount = 0
            total_pieces = npc * len(groups)
            issued = [0]

            def issue_upto(n):
                while issued[0] <= min(n, total_pieces - 1):
                    load_piece(issued[0] % npc, pieces[issued[0] % 2])
                    issued[0] += 1

            issue_upto(1)
            prep_group(xgs[0], groups[0][1], groups[0][2], 1, hTs[0], scr)
            for gi in range(len(groups)):
                r0, ntl, mi = groups[gi]
                ntok = ntl * 128
                xg = xgs[gi % 2]
                hT = hTs[gi % 2]
                for pc in range(npc):
                    pt = pieces[pcount % 2]
                    issue_upto(pcount + 1)
                    pcount += 1
                    if pc == 2 and gi + 1 < len(groups):
                        load_group(gi + 1)
                    if pc == 4 and gi + 1 < len(groups):
                        fill_next = prep_gen(xgs[(gi + 1) % 2], groups[gi + 1][1], groups[gi + 1][2], 1, hTs[(gi + 1) % 2], scr)
                        next(fill_next, None)
                    nj = 4 if pc < 5 else 2
                    for jj in range(nj):
                        j = pc * 4 + jj
                        ba, bb = bank(), bank()
                        for hh, bk in ((0, ba), (1, bb)):
                            for k in range(KC):
                                K.op(PE, lambda e: e.matmul(bk[:, 0:ntok], pt[:, k, hh, jj * 128:(jj + 1) * 128], hT[:, k, 0:ntok],
                                                            start=(k == 0), stop=(k == KC - 1)),
                                     reads=[pt.b, hT.b], writes=[bk.b], inc=(k == KC - 1))
                        s_ = sa[j % 2]
                        K.op(ACT, lambda e: e.activation(out=s_[:, 0:ntok], in_=ba[:, 0:ntok], func=AF.Silu),
                             reads=[ba.b], writes=[s_.b])
                        K.op(DVE, lambda e: e.tensor_tensor(out=act[:, j, 0:ntok], in0=bb[:, 0:ntok], in1=s_[:, 0:ntok], op=ALU.mult),
                             reads=[bb.b, s_.b], writes=[act.b])

                def wout_gen():
                    for tl in range(ntl):
                        for hf in range(2):
                            yb = bank()
                            for k in range(22):
                                K.op(PE, lambda e: e.matmul(yb[:, :], act[:, k, tl * 128:(tl + 1) * 128], woutb[:, k, hf * 512:(hf + 1) * 512],
                                                            start=(k == 0), stop=(k == 21)),
                                     reads=[act.b, woutb.b], writes=[yb.b], inc=(k == 21))
                            residual(xg, tl, hf, yb, gtbc[mi][1], tmp[hf])
                            yield

                issue_upto(pcount + 1)
                fill = fill_next if gi + 1 < len(groups) else None
                interleave(wout_gen(), fill, k=2)
                bg_cast(3)
                if not last:
                    for tl in range(ntl):
                        K.dma(POOL, XD[r0 + tl * 128: r0 + (tl + 1) * 128, :], xg[:, tl, :], reads=[xg.b])
                else:
                    K.op(DVE, lambda e: e.memset(fss[:, 0:ntl], 0.0), writes=[fss.b])
                    for tl in range(ntl):
                        K.op(ACT, lambda e: e.activation(out=fjunk[:], in_=xg[:, tl, :], func=AF.Square, accum_out=fss[:, tl:tl + 1]),
                             reads=[xg.b], writes=[fjunk.b, fss.b])
                    K.op(ACT, lambda e: e.activation(out=fln[:, 0:ntl], in_=fss[:, 0:ntl], func=AF.Ln, scale=1.0 / D, bias=EPS),
                         reads=[fss.b], writes=[fln.b])
                    K.op(ACT, lambda e: e.activation(out=frs[:, 0:ntl], in_=fln[:, 0:ntl], func=AF.Exp, scale=-0.5),
                         reads=[fln.b], writes=[frs.b])
                    for tl in range(ntl):
                        o_ = fo[tl % 2]
                        K.op(DVE, lambda e: e.scalar_tensor_tensor(out=o_[:], in0=xg[:, tl, :], scalar=frs[:, tl:tl + 1], in1=fg[:],
                                                                   op0=ALU.mult, op1=ALU.mult),
                             reads=[xg.b, frs.b, fg.b], writes=[o_.b])
                        K.dma(POOL, out[r0 + tl * 128: r0 + (tl + 1) * 128, :], o_[:], reads=[o_.b])
            K.barrier()

    def gla_front(dirn, qT, kT, ktm, sp_, S, n):
        lt = ltf if dirn == 0 else ltb
        msk = mskf if dirn == 0 else mskb
        E2, E1T, E2T, ke, qeA, qeB, keT, ATm = S["work"][n % 2]
        sp_ap, sp_b = sp_
        fb = banks[6]
        v3 = lambda t_: t_[:].rearrange("p (h t) -> p h t", h=H)
        K.op(PE, lambda e: e.matmul(fb[:, :], lt[:], sp_ap, start=True, stop=True), reads=[CONST, sp_b], writes=[fb.b])
        K.op(ACT, lambda e: e.activation(out=E2[:], in_=fb[:, :], func=AF.Exp, scale=1.0 / 16), reads=[fb.b], writes=[E2.b])
        K.op(DVE, lambda e: e.tensor_tensor(out=ke[:], in0=ktm[0], in1=E2[:], op=ALU.mult), reads=[ktm[1], E2.b], writes=[ke.b])
        yield
        for h in range(H):
            K.op(PE, lambda e: e.matmul(fb[:, h * 128:(h + 1) * 128], sp_ap[:, h * 128:(h + 1) * 128], lt[:], start=True, stop=True),
                 reads=[CONST, sp_b], writes=[fb.b], inc=(h == H - 1))
        K.op(ACT, lambda e: e.activation(out=E1T[:], in_=fb[:, :], func=AF.Exp, scale=-1.0 / 16), reads=[fb.b], writes=[E1T.b])
        K.op(ACT, lambda e: e.activation(out=E2T[:], in_=fb[:, :], func=AF.Exp, scale=1.0 / 16), reads=[fb.b], writes=[E2T.b])
        yield
        for c, qe in ((0, qeA), (1, qeB)):
            K.op(DVE, lambda e: e.scalar_tensor_tensor(out=v3(qe)[:, :, c * 64:(c + 1) * 64], in0=qT[0](c * 64, (c + 1) * 64), scalar=float(DK) ** -0.5,
                                                       in1=v3(E1T)[:, :, c * 64:(c + 1) * 64], op0=ALU.mult, op1=ALU.mult),
                 reads=[qT[1], E1T.b], writes=[qe.b])
        K.op(DVE, lambda e: e.tensor_tensor(out=v3(keT), in0=kT[0](0, 128), in1=v3(E2T), op=ALU.mult),
             reads=[kT[1], E2T.b], writes=[keT.b])
        yield
        for h in range(H):
            hs = slice(h * 128, (h + 1) * 128)
            K.op(PE, lambda e: e.matmul(fb[:, h * 128: h * 128 + 64], keT[:, hs], qeA[:, h * 128: h * 128 + 64], start=True, stop=True),
                 reads=[keT.b, qeA.b], writes=[fb.b], inc=False)
            K.op(PE, lambda e: e.matmul(fb[:, h * 128 + 64: h * 128 + 128], keT[:, hs], qeB[:, h * 128 + 64: h * 128 + 128], start=True, stop=True),
                 reads=[keT.b, qeB.b], writes=[fb.b], inc=(h == H - 1))
        K.op(DVE, lambda e: e.tensor_tensor(out=ATm[:], in0=fb[:, :], in1=msk[:].rearrange("p h t -> p (h t)"), op=ALU.mult),
             reads=[fb.b, CONST], writes=[ATm.b])
        yield

    def gla_chunks(dirn, vbf, S, n, obanks):
        E2, E1T, E2T, ke, qeA, qeB, keT, ATm = S["work"][n % 2]
        Tst, Sbf, dec = S["T"], S["Sbf"], S["dec"]
        for h in range(H):
            ob = obanks[h]
            K.op(PE, lambda e: e.matmul(ob[:, 0:256], ATm[:, h * 128:(h + 1) * 128], vbf[:, h * 256:(h + 1) * 256], start=True, stop=False),
                 reads=[ATm.b, vbf.b], writes=[ob.b], inc=(h == H - 1))
        chunks = (0, 1) if dirn == 0 else (1, 0)
        ub = banks[4]
        for ci, c in enumerate(chunks):
            cs = slice(c * 64, (c + 1) * 64)
            qe = qeA if c == 0 else qeB
            lastcol = (c * 64 + 63) if dirn == 0 else (c * 64)
            dprev = dec[S["d"] % 2]
            dnew = dec[(S["d"] + 1) % 2]
            S["d"] += 1
            for h in range(H):
                K.op(ACT, lambda e: e.activation(out=Sbf[:, h, :], in_=Tst[:, h, :], func=AF.Identity, scale=dprev[:, h:h + 1]),
                     reads=[Tst.b, dprev.b], writes=[Sbf.b])
            K.op(POOL, lambda e: e.tensor_copy(dnew[:, :], E1T[:].rearrange("p (h t) -> p h t", h=H)[:, :, lastcol]),
                 reads=[E1T.b], writes=[dnew.b])
            yield
            for pr in range(2):
                for h in (2 * pr, 2 * pr + 1):
                    K.op(PE, lambda e: e.matmul(ub[:, (h % 2) * 256:(h % 2 + 1) * 256], ke[cs, h * 128:(h + 1) * 128], vbf[cs, h * 256:(h + 1) * 256],
                                                start=True, stop=True), reads=[ke.b, vbf.b], writes=[ub.b], inc=(h % 2 == 1))
                for h in (2 * pr, 2 * pr + 1):
                    ob = obanks[h]
                    K.op(PE, lambda e: e.matmul(ob[:, 0:256], qe[:, h * 128:(h + 1) * 128], Sbf[:, h, :], start=False, stop=(ci == 1)),
                         reads=[qe.b, Sbf.b], writes=[ob.b], inc=True)
                for h in (2 * pr, 2 * pr + 1):
                    K.op(DVE, lambda e: e.scalar_tensor_tensor(out=Tst[:, h, :], in0=Tst[:, h, :], scalar=dprev[:, h:h + 1],
                                                               in1=ub[:, (h % 2) * 256:(h % 2 + 1) * 256], op0=ALU.mult, op1=ALU.add),
                         reads=[Tst.b, dprev.b, ub.b], writes=[Tst.b])
                yield

    def gla_state(st, tag):
        work = []
        for q in range(2):
            work.append((sb(f"{tag}E2{q}", [128, 512], stack=st), sb(f"{tag}E1T{q}", [128, 512], stack=st),
                         sb(f"{tag}E2T{q}", [128, 512], stack=st), sb(f"{tag}ke{q}", [128, 512], BF16, stack=st),
                         sb(f"{tag}qeA{q}", [128, 512], BF16, stack=st), sb(f"{tag}qeB{q}", [128, 512], BF16, stack=st),
                         sb(f"{tag}keT{q}", [128, 512], BF16, stack=st),
                         sb(f"{tag}ATm{q}", [128, 512], BF16, stack=st)))
        S = {"work": work, "n": 0, "d": 0,
             "T": sb(f"{tag}T", [128, H, DV], stack=st), "Sbf": sb(f"{tag}Sbf", [128, H, DV], BF16, stack=st),
             "dec": [sb(f"{tag}dec{q}", [128, H], stack=st) for q in range(2)]}
        for q in range(2):
            K.op(POOL, lambda e: e.memset(work[q][4][:], 0.0), writes=[work[q][4].b])
            K.op(POOL, lambda e: e.memset(work[q][5][:], 0.0), writes=[work[q][5].b])
        K.op(POOL, lambda e: e.memset(S["T"][:], 0.0), writes=[S["T"].b])
        K.op(POOL, lambda e: e.memset(S["dec"][0][:], 1.0), writes=[S["dec"][0].b])
        K.op(POOL, lambda e: e.memset(S["dec"][1][:], 1.0), writes=[S["dec"][1].b])
        return S

    def softplus_neg(dst_ap, dst_b, pre_bank, etmp):
        K.op(ACT, lambda e: e.activation(out=etmp[:], in_=pre_bank[:, :], func=AF.Exp, scale=-1.0), reads=[pre_bank.b], writes=[etmp.b])
        K.op(ACT, lambda e: e.activation(out=dst_ap, in_=etmp[:], func=AF.Ln, bias=1.0), reads=[etmp.b], writes=[dst_b])

    def run_pipeline3(P, F_of, C_of, ntile, psteps=3):
        state = {"done": -1, "alive": P is not None}

        def advance(upto, maxsteps):
            steps = 0
            while state["alive"] and state["done"] < upto and (maxsteps is None or steps < maxsteps):
                r = next(P, _SENT)
                if r is _SENT:
                    state["alive"] = False
                    break
                if r is not None:
                    state["done"] = r[1]
                steps += 1

        if P is not None:
            advance(min(1, ntile - 1), None)
        drain(F_of(0))
        for n in range(ntile):
            f = F_of(n + 1) if n + 1 < ntile else None
            for _ in C_of(n):
                if f is not None and next(f, _SENT) is _SENT:
                    f = None
                if P is not None:
                    advance(min(n + 2, ntile - 1), psteps)
            if f is not None:
                drain(f)
            if P is not None:
                advance(min(n + 2, ntile - 1), None)
        if P is not None:
            for _ in P:
                pass

    def gla_phaseA(i):
        j = i // 2
        pbank_rr = [0]

        def pbank():
            b = banks[(5, 7)[pbank_rr[0] % 2]]
            pbank_rr[0] += 1
            return b

        with ExitStack() as st:
            winb = sb("a_win", [128, KC, GIN], BF16, stack=st)
            wv = gla_w_in_b[j].rearrange("(k p) n -> p k n", p=128)
            for q in range(4):
                K.dma(SP, winb[:, 2 * q:2 * q + 2, :], wv[:, 2 * q:2 * q + 2, :], reads=WB[f"gin{j}"], writes=[winb.b])
            wstage = sb("a_wst", [17, 2, 512], stack=st)
            wab = sb("a_wab", [17, 2, 512], BF16, stack=st)
            for dd, (wa, ba) in enumerate(((gla_wa_f, gla_ba_f), (gla_wa_b, gla_ba_b))):
                K.dma(SP, wstage[0:16, dd, :], wa[j], writes=[wstage.b])
                K.dma(SP, wstage[16:17, dd, :], ba[j:j + 1, :], writes=[wstage.b])
            K.op(DVE, lambda e: e.tensor_copy(wab[:], wstage[:]), reads=[wstage.b], writes=[wab.b])
            xgs = [sb(f"a_xg{q}", [128, 2, D], stack=st) for q in range(2)]
            hT = sb("a_hT", [128, KC, 256], BF16, stack=st)
            qkT = [sb(f"a_qkT{q}", [128, 8, 256], stack=st) for q in range(2)]
            stg = [sb(f"a_stg{q}", [128, 3072], stack=st) for q in range(3)]
            vbf = [sb(f"a_v{q}", [128, D], BF16, stack=st) for q in range(3)]
            spf = [sb(f"a_spf{q}", [128, 512], stack=st) for q in range(3)]
            etmp = sb("a_etmp", [128, 512], stack=st)
            gT = [sb(f"a_gT{q}", [17, 2, 128], BF16, stack=st) for q in range(3)]
            for q in range(3):
                K.op(POOL, lambda e: e.memset(gT[q][:], 1.0), writes=[gT[q].b])
            scr = prep_scratch(st)
            S = gla_state(st, "a_")
            groups = [([0, 1], 1)] + [([2 + g * 2 + t for t in range(2)], 0) for g in range(16)]
            seq = [(gi, tl, stile) for gi, (tiles, mi) in enumerate(groups) for tl, stile in enumerate(tiles)]

            def load_group(gi):
                tiles, mi = groups[gi]
                xg = xgs[gi % 2]
                for tl, stile in enumerate(tiles):
                    load_tile(xg, lambda p0, n, tl=tl: xg[p0:p0 + n, tl, :], xsrc(i, stile))

            def P_stream():
                n = 0
                load_group(0)
                for gi, (tiles, mi) in enumerate(groups):
                    ntl = len(tiles)
                    ntok = ntl * 128
                    xg = xgs[gi % 2]
                    if gi + 1 < len(groups):
                        load_group(gi + 1)
                    yield from prep_gen(xg, ntl, mi, 0, hT, scr, bankfn=pbank)
                    qk = qkT[gi % 2]
                    for mc in range(8):
                        bk = pbank()
                        for k in range(KC):
                            K.op(PE, lambda e: e.matmul(bk[:, 0:ntok], winb[:, k, mc * 128:(mc + 1) * 128], hT[:, k, 0:ntok],
                                                        start=(k == 0), stop=(k == KC - 1)), reads=[winb.b, hT.b], writes=[bk.b], inc=(k == KC - 1))
                        if mc % 2 == 0:
                            K.op(ACT, lambda e: e.copy(qk[:, mc, 0:ntok], bk[:, 0:ntok]), reads=[bk.b], writes=[qk.b])
                        else:
                            K.op(DVE, lambda e: e.tensor_copy(qk[:, mc, 0:ntok], bk[:, 0:ntok]), reads=[bk.b], writes=[qk.b])
                        yield
                    pos0 = tiles[0] * 128
                    K.dma(POOL, SP_FM.rearrange("e d t -> d e t")[:, :, pos0:pos0 + ntok], qk[:, :, 0:ntok], reads=[qk.b])
                    for tl, stile in enumerate(tiles):
                        sg = stg[n % 3]
                        v_ = vbf[n % 3]
                        sf = spf[n % 3]
                        g_ = gT[n % 3]
                        ts = slice(tl * 128, (tl + 1) * 128)
                        for cb in range(5):
                            bk = pbank()
                            c0 = 512 + cb * 512
                            for k in range(KC):
                                K.op(PE, lambda e: e.matmul(bk[:, :], hT[:, k, ts], winb[:, k, c0:c0 + 512], start=(k == 0), stop=(k == KC - 1)),
                                     reads=[winb.b, hT.b], writes=[bk.b], inc=(k == KC - 1))
                            if cb == 0:
                                K.op(ACT, lambda e: e.copy(sg[:, 0:512], bk[:, :]), reads=[bk.b], writes=[sg.b])
                            elif cb in (1, 2):
                                K.op(DVE, lambda e: e.tensor_copy(v_[:, (cb - 1) * 512: cb * 512], bk[:, :]), reads=[bk.b], writes=[v_.b])
                            else:
                                K.op(ACT, lambda e: e.copy(sg[:, 512 + (cb - 3) * 512: 512 + (cb - 2) * 512], bk[:, :]), reads=[bk.b], writes=[sg.b])
                            yield
                        bk = pbank()
                        for dd in range(2):
                            c0 = 3072 + dd * 16
                            for k in range(KC):
                                K.op(PE, lambda e: e.matmul(bk[0:16, dd * 128:(dd + 1) * 128], winb[:, k, c0:c0 + 16], hT[:, k, ts],
                                                            start=(k == 0), stop=(k == KC - 1)), reads=[winb.b, hT.b], writes=[bk.b],
                                     inc=(k == KC - 1 and dd == 1))
                        K.op(DVE, lambda e: e.tensor_copy(g_[0:16, :, :].rearrange("p a t -> p (a t)"), bk[0:16, 0:256]), reads=[bk.b], writes=[g_.b])
                        yield
                        for dd in range(2):
                            pb = pbank()
                            K.op(PE, lambda e: e.matmul(pb[:, :], g_[0:17, dd, :], wab[0:17, dd, :], start=True, stop=True),
                                 reads=[g_.b, wab.b], writes=[pb.b])
                            if dd == 0:
                                softplus_neg(sf[:], sf.b, pb, etmp)
                            else:
                                softplus_neg(sg[:, 1536:2048], sg.b, pb, etmp)
                            yield
                        yield ("tile", n)
                        n += 1

            def F_of(n):
                gi, tl, stile = seq[n]
                qk = qkT[gi % 2]
                sg = stg[n % 3]
                sf = spf[n % 3]
                yield from gla_front(0, (lambda c0, c1: qk[:, 0:4, tl * 128 + c0: tl * 128 + c1], qk.b),
                                     (lambda c0, c1: qk[:, 4:8, tl * 128 + c0: tl * 128 + c1], qk.b),
                                     (sg[:, 0:512], sg.b), (sf[:], sf.b), S, n)

            def C_of(n):
                gi, tl, stile = seq[n]
                sg = stg[n % 3]
                v_ = vbf[n % 3]
                ob = [banks[0], banks[1], banks[2], banks[3]]
                yield from gla_chunks(0, v_, S, n, ob)
                for h in range(H):
                    if h % 2 == 0:
                        K.op(ACT, lambda e: e.copy(sg[:, 2048 + h * 256: 2048 + (h + 1) * 256], ob[h][:, 0:256]), reads=[ob[h].b], writes=[sg.b])
                    else:
                        K.op(DVE, lambda e: e.tensor_copy(sg[:, 2048 + h * 256: 2048 + (h + 1) * 256], ob[h][:, 0:256]), reads=[ob[h].b], writes=[sg.b])
                pos = stile * 128
                K.dma(POOL, SP_TM[pos:pos + 128, :], sg[:], reads=[sg.b])
                K.dma(POOL, SP_V[pos:pos + 128, :], v_[:], reads=[v_.b])
                bg_cast(1)
                yield

            run_pipeline3(P_stream(), F_of, C_of, len(seq))
            K.barrier()

    def gla_phaseB(i):
        j = i // 2
        obank_rr = [0]

        def obank():
            b = banks[(5, 7)[obank_rr[0] % 2]]
            obank_rr[0] += 1
            return b

        with ExitStack() as st:
            woutb = sb("b_wout", [128, KC, D], BF16, stack=st)
            K.dma(SP, woutb[:], gla_w_out_b[j].rearrange("(k p) n -> p k n", p=128), reads=WB[f"gout{j}"], writes=[woutb.b])
            gnb = sb("b_gnb", [128, D], stack=st)
            grow = sb("b_grow", [1, D], stack=st)
            K.dma(SP, grow[0:1, :], gla_norm_g[j:j + 1, :], writes=[grow.b])
            for hf in range(2):
                bk = obank()
                K.op(PE, lambda e: e.matmul(bk[:, :], ones_f[0:1, :], grow[0:1, hf * 512:(hf + 1) * 512], start=True, stop=True),
                     reads=[grow.b, CONST], writes=[bk.b])
                K.op(ACT, lambda e: e.copy(gnb[:, hf * 512:(hf + 1) * 512], bk[:, :]), reads=[bk.b], writes=[gnb.b])
            qkT = [sb(f"b_qkT{q}", [128, 8, 256], stack=st) for q in range(2)]
            NB_ = 4
            stg = [sb(f"b_stg{q}", [128, 3072], stack=st) for q in range(NB_)]
            vbf = [sb(f"b_v{q}", [128, D], BF16, stack=st) for q in range(NB_)]
            xt = [sb(f"b_x{q}", [128, 1, D], stack=st) for q in range(3)]
            osum = [sb(f"b_os{q}", [128, D], stack=st) for q in range(2)]
            gr = [sb(f"b_gr{q}", [128, D], stack=st) for q in range(2)]
            og = [sb(f"b_og{q}", [128, D], stack=st) for q in range(2)]
            ogT = [sb(f"b_ogT{q}", [128, KC, 128], BF16, stack=st) for q in range(2)]
            hss = [sb(f"b_hss{q}", [128, 3, H], stack=st) for q in range(2)]
            junk = sb("b_junk", [128, DV], BF16, stack=st)
            tmp = [sb(f"b_tmp{q}", [128, 512], stack=st) for q in range(2)]
            S = gla_state(st, "b_")
            groups = [([1, 0], 1)] + [([2 + g * 2 + t for t in (1, 0)], 0) for g in range(15, -1, -1)]

            def load_group(gi):
                tiles, mi = groups[gi]
                lo = min(tiles) * 128
                n = len(tiles) * 128
                qk = qkT[gi % 2]
                K.dma(SP, qk[:, 0:4, 0:n], SP_FM.rearrange("e d t -> d e t")[:, 0:4, lo:lo + n], writes=[qk.b])
                K.dma(SP, qk[:, 4:8, 0:n], SP_FM.rearrange("e d t -> d e t")[:, 4:8, lo:lo + n], writes=[qk.b])

            seq = [(gi, stile) for gi, (tiles, mi) in enumerate(groups) for stile in tiles]

            def load_tile_b(n):
                gi, stile = seq[n]
                pos = stile * 128
                K.dma(SP, stg[n % NB_][:], SP_TM[pos:pos + 128, :], writes=[stg[n % NB_].b])
                K.dma(SP, vbf[n % NB_][:], SP_V[pos:pos + 128, :], writes=[vbf[n % NB_].b])

            def load_x(n):
                gi, stile = seq[n]
                load_tile(xt[n % 3], lambda p0, nn, n=n: xt[n % 3][p0:p0 + nn, 0, :], xsrc(i, stile))

            def F_of(n):
                gi, stile = seq[n]
                tiles, mi = groups[gi]
                tl = stile - min(tiles)
                qk = qkT[gi % 2]
                sg = stg[n % NB_]
                yield from gla_front(1, (lambda c0, c1: qk[:, 0:4, tl * 128 + c0: tl * 128 + c1], qk.b),
                                     (lambda c0, c1: qk[:, 4:8, tl * 128 + c0: tl * 128 + c1], qk.b),
                                     (sg[:, 0:512], sg.b), (sg[:, 1536:2048], sg.b), S, n)

            def core(n):
                sg = stg[n % NB_]
                ob = [banks[0], banks[1], banks[2], banks[3]]
                yield from gla_chunks(1, vbf[n % NB_], S, n, ob)
                os_ = osum[n % 2]
                for q in range(H):
                    K.op(DVE, lambda e: e.tensor_tensor(out=os_[:, q * 256:(q + 1) * 256], in0=ob[q][:, 0:256],
                                                        in1=sg[:, 2048 + q * 256: 2048 + (q + 1) * 256], op=ALU.add),
                         reads=[ob[q].b, sg.b], writes=[os_.b])
                yield

            def outp(n):
                gi, stile = seq[n]
                tiles, mi = groups[gi]
                sg = stg[n % NB_]
                os_ = osum[n % 2]
                hs_ = hss[n % 2]
                g_ = gr[n % 2]
                og_ = og[n % 2]
                ogT_ = ogT[n % 2]
                x_ = xt[n % 3]
                K.op(ACT, lambda e: e.activation(out=g_[:], in_=sg[:, 512:1536], func=AF.Silu), reads=[sg.b], writes=[g_.b])
                K.op(POOL, lambda e: e.tensor_tensor(out=g_[:], in0=g_[:], in1=gnb[:], op=ALU.mult), reads=[g_.b, gnb.b], writes=[g_.b])
                yield
                K.op(DVE, lambda e: e.memset(hs_[:, 0, :], 0.0), writes=[hs_.b])
                for h in range(H):
                    K.op(ACT, lambda e: e.activation(out=junk[:], in_=os_[:, h * DV:(h + 1) * DV], func=AF.Square, accum_out=hs_[:, 0, h:h + 1]),
                         reads=[os_.b], writes=[junk.b, hs_.b])
                K.op(ACT, lambda e: e.activation(out=hs_[:, 1, :], in_=hs_[:, 0, :], func=AF.Ln, scale=1.0 / DV, bias=EPS),
                     reads=[hs_.b], writes=[hs_.b])
                K.op(ACT, lambda e: e.activation(out=hs_[:, 2, :], in_=hs_[:, 1, :], func=AF.Exp, scale=-0.5), reads=[hs_.b], writes=[hs_.b])
                yield
                yield
                for h in range(H):
                    hs = slice(h * DV, (h + 1) * DV)
                    K.op(DVE, lambda e: e.scalar_tensor_tensor(out=og_[:, hs], in0=os_[:, hs], scalar=hs_[:, 2, h:h + 1], in1=g_[:, hs],
                                                               op0=ALU.mult, op1=ALU.mult), reads=[os_.b, hs_.b, g_.b], writes=[og_.b])
                yield
                yield
                for hf in range(2):
                    bk = obank()
                    for kk in range(4):
                        k = hf * 4 + kk
                        K.op(PE, lambda e: e.transpose(bk[:, kk * 128:(kk + 1) * 128], og_[:, k * 128:(k + 1) * 128], ident[:]),
                             reads=[og_.b, CONST], writes=[bk.b], inc=(kk == 3))
                    K.op(ACT, lambda e: e.copy(ogT_[:, hf * 4:(hf + 1) * 4, :].rearrange("p k t -> p (k t)"), bk[:, :]),
                         reads=[bk.b], writes=[ogT_.b])
                yield
                for hf in range(2):
                    yb = obank()
                    for k in range(KC):
                        K.op(PE, lambda e: e.matmul(yb[:, :], ogT_[:, k, :], woutb[:, k, hf * 512:(hf + 1) * 512], start=(k == 0), stop=(k == KC - 1)),
                             reads=[ogT_.b, woutb.b], writes=[yb.b], inc=(k == KC - 1))
                    residual(x_, 0, hf, yb, gtbc[mi][0], tmp[hf])
                    yield
                store_tile(x_, lambda p0, nn: x_[p0:p0 + nn, 0, :], xdst(i, stile))

            ntile = len(seq)
            load_group(0)
            load_tile_b(0)
            if ntile > 1:
                if seq[1][0] != seq[0][0]:
                    load_group(seq[1][0])
                load_tile_b(1)
            load_x(0)
            drain(F_of(0))
            for n in range(ntile):
                if n + 1 < ntile:
                    load_x(n + 1)
                if n + 2 < ntile:
                    if seq[n + 2][0] != seq[n + 1][0]:
                        load_group(seq[n + 2][0])
                    load_tile_b(n + 2)
                f = F_of(n + 1) if n + 1 < ntile else None
                o = outp(n - 1) if n > 0 else None
                for _ in core(n):
                    if f is not None and next(f, _SENT) is _SENT:
                        f = None
                    if o is not None and next(o, _SENT) is _SENT:
                        o = None
                if f is not None:
                    drain(f)
                if o is not None:
                    drain(o)
            drain(outp(ntile - 1))
            K.barrier()

    def conv_phase(i, last):
        j = i // 2
        colb = colb_all[j]
        wdw = wdw_all[j]
        with ExitStack() as st:
            groups = [([2 + g * 4 + t for t in range(4)], 0) for g in range(8)]
            if not last:
                groups = [([0, 1], 1)] + groups
            identb = sb("c_identb", [128, 128], BF16, stack=st)
            K.op(DVE, lambda e: e.tensor_copy(identb[:], ident[:]), reads=[CONST], writes=[identb.b])
            diag = sb("c_diag", [128, KC, CW, 128], BF16, stack=st)

            def diag_gen():
                for c in range(KC):
                    for tp in range(CW):
                        if tp % 2 == 0:
                            K.op(ACT, lambda e: e.activation(out=diag[:, c, tp, :], in_=identb[:], func=AF.Identity, scale=wdw[:, c, tp:tp + 1]),
                                 reads=[identb.b, wdw.b], writes=[diag.b])
                        else:
                            K.op(DVE, lambda e: e.tensor_scalar(out=diag[:, c, tp, :], in0=identb[:], scalar1=wdw[:, c, tp:tp + 1], scalar2=None, op0=ALU.mult),
                                 reads=[identb.b, wdw.b], writes=[diag.b])
                        if tp % 4 == 3:
                            yield

            with ExitStack() as st1:
                w1 = sb("c_w1", [128, KC, 2 * D], BF16, stack=st1)
                wv = pw1_b[j].rearrange("(k p) n -> p k n", p=128)
                for q in range(4):
                    K.dma(SP, w1[:, 2 * q:2 * q + 2, :], wv[:, 2 * q:2 * q + 2, :], reads=WB[f"pw1{j}"], writes=[w1.b])
                xgs = [sb(f"c_xg{q}", [128, 4, D], stack=st1) for q in range(2)]
                hTs = [sb(f"c_hT{q}", [128, KC, 512], BF16, stack=st1) for q in range(2)]
                ug = [sb(f"c_ug{q}", [128, KC, 512], BF16, stack=st1) for q in range(2)]
                sig = [sb(f"c_sig{q}", [128, 512], stack=st1) for q in range(2)]
                scr = prep_scratch(st1)
                dg = diag_gen()

                def load_group(gi):
                    tiles, mi = groups[gi]
                    xg = xgs[gi % 2]
                    for tl, stile in enumerate(tiles):
                        load_tile(xg, lambda p0, n, tl=tl: xg[p0:p0 + n, tl, :], xsrc(i, stile))

                load_group(0)
                if len(groups) > 1:
                    load_group(1)
                prep_group(xgs[0], len(groups[0][0]), groups[0][1], 0, hTs[0], scr)
                for gi, (tiles, mi) in enumerate(groups):
                    ntl = len(tiles)
                    ntok = ntl * 128
                    xg = xgs[gi % 2]
                    hT = hTs[gi % 2]
                    fillp = None
                    if gi + 1 < len(groups):
                        fillp = prep_gen(xgs[(gi + 1) % 2], len(groups[gi + 1][0]), groups[gi + 1][1], 0, hTs[(gi + 1) % 2], scr)
                    u_ = ug[gi % 2]
                    for c in range(KC):
                        if fillp is not None and c >= 1:
                            if next(fillp, _SENT) is _SENT:
                                fillp = None
                        ba, bb = bank(), bank()
                        for hh, bk in ((0, ba), (1, bb)):
                            for k in range(KC):
                                K.op(PE, lambda e: e.matmul(bk[:, 0:ntok], w1[:, k, hh * D + c * 128: hh * D + (c + 1) * 128], hT[:, k, 0:ntok],
                                                            start=(k == 0), stop=(k == KC - 1)), reads=[w1.b, hT.b], writes=[bk.b], inc=(k == KC - 1))
                        s_ = sig[c % 2]
                        K.op(ACT, lambda e: e.activation(out=s_[:, 0:ntok], in_=bb[:, 0:ntok], func=AF.Sigmoid, bias=colb[:, 0, KC + c:KC + c + 1]),
                             reads=[bb.b, colb.b], writes=[s_.b])
                        K.op(DVE, lambda e: e.scalar_tensor_tensor(out=u_[:, c, 0:ntok], in0=ba[:, 0:ntok], scalar=colb[:, 0, c:c + 1],
                                                                   in1=s_[:, 0:ntok], op0=ALU.add, op1=ALU.mult),
                             reads=[ba.b, colb.b, s_.b], writes=[u_.b])
                        next(dg, None)
                    if fillp is not None:
                        drain(fillp)
                    if gi + 2 < len(groups):
                        load_group(gi + 2)
                    pos0 = tiles[0] * 128
                    K.dma(POOL, UD.rearrange("c p t -> p c t")[:, :, pos0:pos0 + ntok], u_[:, :, 0:ntok], reads=[u_.b])
                drain(dg)
                K.barrier()
            with ExitStack() as st2:
                w2 = sb("c_w2", [128, KC, D], BF16, stack=st2)
                K.dma(SP, w2[:], pw2_b[j].rearrange("(k p) n -> p k n", p=128), reads=WB[f"pw2{j}"], writes=[w2.b])
                b2row = sb("c_b2row", [1, D], stack=st2)
                b2bf = sb("c_b2bf", [1, D], BF16, stack=st2)
                K.dma(SP, b2row[0:1, :], conv_b_pw2[j:j + 1, :], writes=[b2row.b])
                K.op(DVE, lambda e: e.tensor_copy(b2bf[:], b2row[:]), reads=[b2row.b], writes=[b2bf.b])
                ugs = [sb(f"c_uh{q}", [128, KC, 512 + 32], BF16, stack=st2) for q in range(2)]
                xg1 = sb("c_xh", [128, 4, D], stack=st2)
                Vs = [sb(f"c_V{q}", [128, KC, 512], stack=st2) for q in range(2)]
                vb = [sb(f"c_vb{q}", [128, 512], BF16, stack=st2) for q in range(2)]
                vq = [sb(f"c_vq{q}", [128, 512], BF16, stack=st2) for q in range(2)]
                mean = sb("c_mean", [128, 512], stack=st2)
                var = sb("c_var", [128, 512], stack=st2)
                rst = sb("c_rst", [128, 512], stack=st2)
                Sact = sb("c_S", [128, KC, 512], BF16, stack=st2)
                tmp = [sb(f"c_tmp{q}", [128, 512], stack=st2) for q in range(2)]
                HALO = CW // 2
                crr = [0]
                yrr = [0]

                def seg_bounds(mi):
                    return (0, CT) if mi == 1 else (CT, CT + L)

                def load_u(gi):
                    tiles, mi = groups[gi]
                    ntok = len(tiles) * 128
                    pos0 = tiles[0] * 128
                    lo_b, hi_b = seg_bounds(mi)
                    lo = max(lo_b, pos0 - HALO)
                    hi = min(hi_b, pos0 + ntok + HALO)
                    u_ = ugs[gi % 2]
                    K.dma(SP, u_[:, :, 16 + lo - pos0: 16 + hi - pos0], UD.rearrange("c p t -> p c t")[:, :, lo:hi], writes=[u_.b])

                def A_gen(gi):
                    tiles, mi = groups[gi]
                    ntok = len(tiles) * 128
                    pos0 = tiles[0] * 128
                    lo_b, hi_b = seg_bounds(mi)
                    u_ = ugs[gi % 2]
                    V = Vs[gi % 2]
                    bm, bq = banks[2 + 2 * (gi % 2)], banks[3 + 2 * (gi % 2)]
                    pend = []
                    for c in range(KC):
                        bk = banks[crr[0] % 2]
                        crr[0] += 1
                        order = [HALO] + [tp for tp in range(CW) if tp != HALO]
                        for n_, tp in enumerate(order):
                            sh = tp - HALO
                            t_lo = max(0, lo_b - (pos0 + sh))
                            t_hi = min(ntok, hi_b - (pos0 + sh))
                            K.op(PE, lambda e: e.matmul(bk[:, t_lo:t_hi], diag[:, c, tp, :], u_[:, c, 16 + t_lo + sh: 16 + t_hi + sh],
                                                        start=(n_ == 0), stop=(n_ == CW - 1)), reads=[diag.b, u_.b], writes=[bk.b], inc=(n_ == CW - 1))
                        while pend:
                            pend.pop(0)()
                        K.op(ACT, lambda e: e.activation(out=V[:, c, 0:ntok], in_=bk[:, 0:ntok], func=AF.Identity, bias=colb[:, 1, c:c + 1]),
                             reads=[bk.b, colb.b], writes=[V.b])
                        vb_, vq_ = vb[c % 2], vq[c % 2]
                        K.op(POOL, lambda e: e.tensor_copy(vb_[:, 0:ntok], V[:, c, 0:ntok]), reads=[V.b], writes=[vb_.b])
                        K.op(ACT, lambda e: e.activation(out=vq_[:, 0:ntok], in_=V[:, c, 0:ntok], func=AF.Square), reads=[V.b], writes=[vq_.b])

                        def stats(c=c, vb_=vb_, vq_=vq_):
                            K.op(PE, lambda e: e.matmul(bm[:, 0:ntok], ones_bf[:], vb_[:, 0:ntok], start=(c == 0), stop=(c == KC - 1)),
                                 reads=[vb_.b, CONST], writes=[bm.b], inc=True)
                            K.op(PE, lambda e: e.matmul(bq[:, 0:ntok], ones_bf[:], vq_[:, 0:ntok], start=(c == 0), stop=(c == KC - 1)),
                                 reads=[vq_.b, CONST], writes=[bq.b], inc=True)

                        pend.append(stats)
                        yield
                    while pend:
                        pend.pop(0)()
                    yield

                def B_gen(gi):
                    tiles, mi = groups[gi]
                    ntl = len(tiles)
                    ntok = ntl * 128
                    V = Vs[gi % 2]
                    bm, bq = banks[2 + 2 * (gi % 2)], banks[3 + 2 * (gi % 2)]
                    xg = xg1
                    for tl, stile in enumerate(tiles):
                        load_tile(xg, lambda p0, n, tl=tl: xg[p0:p0 + n, tl, :], xsrc(i, stile))
                    K.op(ACT, lambda e: e.activation(out=mean[:, 0:ntok], in_=bm[:, 0:ntok], func=AF.Identity, scale=1.0 / D), reads=[bm.b], writes=[mean.b])
                    K.op(DVE, lambda e: e.tensor_tensor(out=var[:, 0:ntok], in0=mean[:, 0:ntok], in1=mean[:, 0:ntok], op=ALU.mult),
                         reads=[mean.b], writes=[var.b])
                    K.op(DVE, lambda e: e.scalar_tensor_tensor(out=var[:, 0:ntok], in0=bq[:, 0:ntok], scalar=1.0 / D, in1=var[:, 0:ntok],
                                                               op0=ALU.mult, op1=ALU.subtract), reads=[bq.b, var.b], writes=[var.b])
                    K.op(ACT, lambda e: e.activation(out=rst[:, 0:ntok], in_=var[:, 0:ntok], func=AF.Ln, bias=EPS), reads=[var.b], writes=[rst.b])
                    K.op(ACT, lambda e: e.activation(out=rst[:, 0:ntok], in_=rst[:, 0:ntok], func=AF.Exp, scale=-0.5), reads=[rst.b], writes=[rst.b])
                    yield
                    for c in range(KC):
                        K.op(DVE, lambda e: e.tensor_tensor(out=V[:, c, 0:ntok], in0=V[:, c, 0:ntok], in1=mean[:, 0:ntok], op=ALU.subtract),
                             reads=[V.b, mean.b], writes=[V.b])
                        K.op(POOL, lambda e: e.tensor_tensor(out=V[:, c, 0:ntok], in0=V[:, c, 0:ntok], in1=rst[:, 0:ntok], op=ALU.mult),
                             reads=[V.b, rst.b], writes=[V.b])
                        K.op(ACT, lambda e: e.activation(out=Sact[:, c, 0:ntok], in_=V[:, c, 0:ntok], func=AF.Silu,
                                                         scale=colb[:, 2, c:c + 1], bias=colb[:, 3, c:c + 1]),
                             reads=[V.b, colb.b], writes=[Sact.b])
                        if c % 2 == 1:
                            yield
                    for _q in range(4):
                        yield
                    for tl, stile in enumerate(tiles):
                        for hf in range(2):
                            yb = banks[6 + yrr[0] % 2]
                            yrr[0] += 1
                            for k in range(KC):
                                K.op(PE, lambda e: e.matmul(yb[:, :], Sact[:, k, tl * 128:(tl + 1) * 128], w2[:, k, hf * 512:(hf + 1) * 512],
                                                            start=(k == 0), stop=False), reads=[Sact.b, w2.b], writes=[yb.b], inc=False)
                            K.op(PE, lambda e: e.matmul(yb[:, :], ones_bf[0:1, :], b2bf[0:1, hf * 512:(hf + 1) * 512], start=False, stop=True),
                                 reads=[b2bf.b, CONST], writes=[yb.b], inc=True)
                            residual(xg, tl, hf, yb, gtbc[mi][0], tmp[hf])
                            yield
                        store_tile(xg, lambda p0, n, tl=tl: xg[p0:p0 + n, tl, :], xdst(i, stile))

                load_u(0)
                if len(groups) > 1:
                    load_u(1)
                drain(A_gen(0))
                for gi in range(len(groups)):
                    if gi + 1 < len(groups):
                        interleave(A_gen(gi + 1), B_gen(gi), k=2)
                    else:
                        drain(B_gen(gi))
                    if gi + 2 < len(groups):
                        load_u(gi + 2)
                K.barrier()

    for i in range(nlayers):
        last = (i == DEPTH - 1)
        mod_phase(i)
        if dbg == "mod":
            K.dma(SP, out[0:128, 0:64], colv[:].rearrange("p v k m -> p (v k m)"), reads=[colv.b])
            K.dma(SP, out[128:256, :], gtbc[0][0][:], reads=[gtbc[0][0].b])
            K.dma(SP, out[256:384, :], gtbc[1][1][:], reads=[gtbc[1][1].b])
            K.dma(SP, out[384:512, 0:16], scT[:].rearrange("p k m -> p (k m)"), reads=[scT.b])
            K.dma(SP, out[512:640, 0:32], gmix[:].rearrange("p l k -> p (l k)"), reads=[gmix.b])
            break
        if i % 2 == 0:
            gla_phaseA(i)
            gla_phaseB(i)
        else:
            conv_phase(i, last)
        if dbg == "mix" and i == nlayers - 1:
            break
        ffn_phase(i, last)

    if dbg == "sptm":
        with ExitStack() as st:
            d_ = sb("dbgs", [128, 3072], stack=st)
            for tt in range(8):
                K.dma(SP, d_[:], SP_TM[tt * 128:(tt + 1) * 128, :], writes=[d_.b])
                for q in range(3):
                    K.dma(SP, out[tt * 384 + q * 128: tt * 384 + (q + 1) * 128, :], d_[:, q * 1024:(q + 1) * 1024], reads=[d_.b])
    elif dbg == "mod":
        pass
    elif nlayers < DEPTH or dbg == "mix":
        with ExitStack() as st:
            dt_ = [sb(f"dbg{q}", [128, 4, D], stack=st) for q in range(2)]
            for g in range(8):
                d_ = dt_[g % 2]
                for tl in range(4):
                    r = g * 512 + tl * 128
                    K.dma(SP, d_[:, tl, :], XD[r:r + 128, :], writes=[d_.b])
                for tl in range(4):
                    r = g * 512 + tl * 128
                    K.dma(SP, out[r:r + 128, :], d_[:, tl, :], reads=[d_.b])
    K.barrier()
    es.close()
    return nc, K


_CACHE = {}


def kernel(**inputs):
    nl = int(inputs.pop("_nlayers", DEPTH))
    if nl not in _CACHE:
        _CACHE[nl] = build(nl)[0]
    nc = _CACHE[nl]
    f32 = lambda a: np.ascontiguousarray(np.asarray(a, dtype=np.float32))
    shared = {}
    for k in ("w_mod", "b_mod", "norm_mix_g", "norm_ffn_g", "gla_w_in", "gla_wa_f", "gla_ba_f", "gla_wa_b", "gla_ba_b",
              "gla_norm_g", "gla_w_out", "conv_w_pw1", "conv_b_pw1", "conv_w_dw", "conv_b_dw", "conv_ln_g", "conv_ln_b",
              "conv_w_pw2", "conv_b_pw2", "ffn_w_in", "ffn_w_out"):
        shared[k] = f32(inputs[k])
    shared["final_norm_g"] = f32(inputs["final_norm_g"]).reshape(1, D)
    shared["c_ctx"] = f32(inputs["c_ctx"]).reshape(1, D)
    x = f32(inputs["x"])
    c = f32(inputs["c"])
    ctx = f32(inputs["ctx"])
    in_maps = []
    for b in range(8):
        m = dict(shared)
        m["x"] = x[b]
        m["ctx"] = ctx[b]
        m["c"] = c[b].reshape(1, D)
        in_maps.append(m)
    res = run_bass_kernel_spmd(nc, in_maps, core_ids=list(range(8)))
    return np.stack([np.asarray(r["out"], dtype=np.float32) for r in res.results], axis=0)
```

```python
import numpy as np
from contextlib import ExitStack
import concourse.bass as bass
import concourse.mybir as mybir
from concourse.bass_utils import run_bass_kernel_spmd

F32 = mybir.dt.float32
BF16 = mybir.dt.bfloat16
AF = mybir.ActivationFunctionType
ALU = mybir.AluOpType
AX = mybir.AxisListType

D = 1024
L = 4096
CT = 256
NTOK = L + CT
H = 4
DK = 128
DV = 256
FH = 2816
GIN = 3104
DEPTH = 4
CW = 31
EPS = 1e-6
KC = 8


class Tok:
    __slots__ = ("sem", "val")

    def __init__(self, sem, val):
        self.sem = sem
        self.val = val


class Buf:
    def __init__(self, name="b"):
        self.name = name
        self.last_w = None
        self.readers = []


class Eng:
    LIMIT = 30000

    def __init__(self, K, name, eng):
        self.K = K
        self.name = name
        self.eng = eng
        self.sem = None
        self.count = 0
        self.nsem = 0
        self.waited = {}
        self.pend_r = []
        self.pend_w = []

    def _newsem(self):
        self.sem = self.K.es.enter_context(self.K.nc.semaphore(f"s_{self.name}_{self.nsem}"))
        self.nsem += 1
        self.count = 0

    def wait(self, tok):
        if tok is None:
            return
        key = id(tok.sem)
        if self.waited.get(key, 0) >= tok.val:
            return
        self.eng.wait_ge(tok.sem, tok.val)
        self.waited[key] = tok.val


class Kern:
    NDMA = 28

    def __init__(self, nc):
        self.nc = nc
        self.es = ExitStack()
        self.pe = Eng(self, "pe", nc.tensor)
        self.act = Eng(self, "act", nc.scalar)
        self.dve = Eng(self, "dve", nc.vector)
        self.pool = Eng(self, "pool", nc.gpsimd)
        self.sp = Eng(self, "sp", nc.sync)
        self.all = (self.pe, self.act, self.dve, self.pool, self.sp)
        self.dma_pools = {}
        self.phase_toks = []
        self.ninst = 0

    def _deps(self, E, reads, writes):
        for b in reads:
            for e in self.all:
                if e is not E and b in e.pend_w:
                    raise RuntimeError(f"pending writer on {b.name}")
            E.wait(b.last_w)
        for b in writes:
            for e in self.all:
                if e is not E and (b in e.pend_w or b in e.pend_r):
                    raise RuntimeError(f"pending access on {b.name}")
            E.wait(b.last_w)
            for r in b.readers:
                E.wait(r)

    def _commit(self, tok, reads, writes):
        for b in writes:
            b.last_w = tok
            b.readers = []
        for b in reads:
            if b not in writes:
                b.readers.append(tok)

    def op(self, E, fn, reads=(), writes=(), inc=True):
        reads = list(reads)
        writes = list(writes)
        self._deps(E, reads, writes)
        ins = fn(E.eng)
        self.ninst += 1
        if inc:
            if E.sem is None or E.count >= Eng.LIMIT:
                E._newsem()
            E.count += 1
            ins.then_inc(E.sem, 1)
            tok = Tok(E.sem, E.count)
            self._commit(tok, reads + E.pend_r, writes + E.pend_w)
            E.pend_r = []
            E.pend_w = []
            return tok
        E.pend_r += reads
        E.pend_w += writes
        return None

    def dma(self, E, out, in_, reads=(), writes=(), track=True, sempool=None):
        reads = list(reads)
        writes = list(writes)
        self._deps(E, reads, writes)
        pool_ = self.dma_pools.setdefault(sempool or E.name, [[], 0])
        nmax = self.NDMA if E is self.sp else 12
        if len(pool_[0]) < nmax:
            sem = self.es.enter_context(self.nc.semaphore(f"s_dma_{sempool or E.name}_{len(pool_[0])}"))
            slot = [sem, 0]
            pool_[0].append(slot)
        else:
            slot = pool_[0][pool_[1] % nmax]
            pool_[1] += 1
            E.wait(Tok(slot[0], slot[1]))
            if slot[1] >= 30000:
                slot[0] = self.es.enter_context(self.nc.semaphore(f"s_dmax_{E.name}_{pool_[1]}"))
                slot[1] = 0
        ins = E.eng.dma_start(out=out, in_=in_)
        self.ninst += 1
        slot[1] += 16
        ins.then_inc(slot[0], 16)
        tok = Tok(slot[0], slot[1])
        self._commit(tok, reads, writes)
        if track:
            self.phase_toks.append(tok)
        return tok

    def barrier(self):
        for t in self.phase_toks:
            self.sp.wait(t)
        self.phase_toks = []
        for e in (self.pe, self.act, self.dve, self.pool):
            assert not e.pend_r and not e.pend_w
            if e.sem is not None:
                self.sp.wait(Tok(e.sem, e.count))
        tok = self.op(self.sp, lambda e: e.nop())
        for e in (self.pe, self.act, self.dve, self.pool):
            e.wait(tok)


class T:
    def __init__(self, t, name):
        self.t = t
        self.b = Buf(name)

    def __getitem__(self, idx):
        return self.t[idx]


def build(nlayers=DEPTH, dbg=None):
    nc = bass.Bass("TRN2", target_bir_lowering=False)
    K = Kern(nc)
    es = K.es

    def din(name, shape):
        return nc.dram_tensor(name, list(shape), F32, kind="ExternalInput").ap()

    x_in = din("x", [L, D])
    ctx_in = din("ctx", [CT, D])
    c_in = din("c", [1, D])
    cctx_in = din("c_ctx", [1, D])
    w_mod = din("w_mod", [DEPTH, D, 6 * D])
    b_mod = din("b_mod", [DEPTH, 6 * D])
    norm_mix_g = din("norm_mix_g", [DEPTH, D])
    norm_ffn_g = din("norm_ffn_g", [DEPTH, D])
    gla_w_in = din("gla_w_in", [2, D, GIN])
    gla_wa_f = din("gla_wa_f", [2, 16, 512])
    gla_ba_f = din("gla_ba_f", [2, 512])
    gla_wa_b = din("gla_wa_b", [2, 16, 512])
    gla_ba_b = din("gla_ba_b", [2, 512])
    gla_norm_g = din("gla_norm_g", [2, D])
    gla_w_out = din("gla_w_out", [2, D, D])
    conv_w_pw1 = din("conv_w_pw1", [2, D, 2 * D])
    conv_b_pw1 = din("conv_b_pw1", [2, 2 * D])
    conv_w_dw = din("conv_w_dw", [2, CW, D])
    conv_b_dw = din("conv_b_dw", [2, D])
    conv_ln_g = din("conv_ln_g", [2, D])
    conv_ln_b = din("conv_ln_b", [2, D])
    conv_w_pw2 = din("conv_w_pw2", [2, D, D])
    conv_b_pw2 = din("conv_b_pw2", [2, D])
    ffn_w_in = din("ffn_w_in", [DEPTH, D, 2 * FH])
    ffn_w_out = din("ffn_w_out", [DEPTH, FH, D])
    final_norm_g = din("final_norm_g", [1, D])
    out = nc.dram_tensor("out", [L, D], F32, kind="ExternalOutput").ap()

    def dscr(name, shape, dt=F32):
        return nc.dram_tensor(name, list(shape), dt, kind="Internal").ap()

    XD = dscr("XD", [NTOK, D])
    gla_w_in_b = dscr("gla_w_in_b", [2, D, GIN], BF16)
    gla_w_out_b = dscr("gla_w_out_b", [2, D, D], BF16)
    pw1_b = dscr("pw1_b", [2, D, 2 * D], BF16)
    pw2_b = dscr("pw2_b", [2, D, D], BF16)
    ffn_in_b = dscr("ffn_in_b", [DEPTH, 6, 128, KC * 2 * 512], BF16)
    ffn_out_b = dscr("ffn_out_b", [DEPTH, FH, D], BF16)
    SP_TM = dscr("SP_TM", [NTOK, 3072])
    SP_V = dscr("SP_V", [NTOK, D], BF16)
    SP_FM = dscr("SP_FM", [8, 128, NTOK])
    UD = dscr("UD", [KC, 128, NTOK], BF16)

    nctr = [0]

    def sb(name, shape, dt=F32, stack=None):
        nctr[0] += 1
        name = f"{name}_{nctr[0]}"
        return T((stack or es).enter_context(nc.sbuf_tensor(name, list(shape), dt)), name)

    PE, ACT, DVE, POOL, SP = K.pe, K.act, K.dve, K.pool, K.sp

    banks = [T(es.enter_context(nc.psum_tensor(f"bank{i}", [128, 512], F32)), f"bank{i}") for i in range(8)]
    bank_rr = [0]
    bank_pool = [list(range(8))]
    obank_rr = [0]

    def bank():
        p = bank_pool[0]
        b = banks[p[bank_rr[0] % len(p)]]
        bank_rr[0] += 1
        return b

    def obank_pair():
        q = obank_rr[0] % 2
        obank_rr[0] += 1
        return [banks[2 * q], banks[2 * q + 1]]

    WB = {}

    def cast(name, dst, src):
        WB[name] = Buf(name)
        K.dma(POOL, dst, src, writes=[WB[name]], track=False)

    pending_casts = []

    def queue_cast(name, dst, src, nchunk):
        rows = src.shape[0]
        assert rows % nchunk == 0
        step = rows // nchunk
        WB[name] = [Buf(f"{name}_{q}") for q in range(nchunk)]
        for q in range(nchunk):
            pending_casts.append((WB[name][q], dst[q * step:(q + 1) * step, :], src[q * step:(q + 1) * step, :]))

    def bg_cast(n=1):
        for _ in range(n):
            if not pending_casts:
                return
            b_, dst, src = pending_casts.pop(0)
            K.dma(POOL, dst, src, writes=[b_], track=False, sempool="cast")

    def casts_for_layer(i):
        j = i // 2
        if i % 2 == 0:
            queue_cast(f"gin{j}", gla_w_in_b[j], gla_w_in[j], 4)
            queue_cast(f"gout{j}", gla_w_out_b[j], gla_w_out[j], 2)
        else:
            queue_cast(f"pw1{j}", pw1_b[j], conv_w_pw1[j], 4)
            queue_cast(f"pw2{j}", pw2_b[j], conv_w_pw2[j], 2)
        WB[f"fin{i}"] = []
        wsrc = ffn_w_in[i].rearrange("(k p) n -> p k n", p=128)
        for pc in range(6):
            w = 512 if pc < 5 else 256
            dstv = ffn_in_b[i, pc].rearrange("p (k h w) -> p k h w", k=KC, h=2)
            for hh in range(2):
                b_ = Buf(f"fin{i}_{pc}_{hh}")
                WB[f"fin{i}"].append(b_)
                pending_casts.append((b_, dstv[:, :, hh, 0:w], wsrc[:, :, hh * FH + pc * 512: hh * FH + pc * 512 + w]))
        queue_cast(f"fout{i}", ffn_out_b[i], ffn_w_out[i], 4)

    casts_for_layer(0)
    bg_cast(4)

    ident = sb("ident", [128, 128])
    ltf = sb("ltf", [128, 128])
    ltb = sb("ltb", [128, 128])
    mskf = sb("mskf", [128, H, 128])
    mskb = sb("mskb", [128, H, 128])
    sel2 = sb("sel2", [2, 2, 128])
    ones_bf = sb("ones_bf", [128, 128], BF16)
    ones_f = sb("ones_f", [128, 128])

    def aff(dst, pattern, cm, cmp, base=0):
        K.op(POOL, lambda e: e.affine_select(out=dst, in_=dst, pattern=pattern, compare_op=cmp, fill=0.0,
                                             base=base, channel_multiplier=cm), writes=[CONST])

    CONST = Buf("const")
    K.op(POOL, lambda e: e.memset(ident[:], 1.0), writes=[CONST])
    aff(ident[:], [[-1, 128]], 1, ALU.is_equal)
    K.op(POOL, lambda e: e.memset(ltf[:], 1.0), writes=[CONST])
    aff(ltf[:], [[1, 128]], -1, ALU.is_ge)
    K.op(POOL, lambda e: e.memset(ltf[0:64, 64:128], 0.0), writes=[CONST])
    K.op(POOL, lambda e: e.memset(ltb[:], 1.0), writes=[CONST])
    aff(ltb[:], [[-1, 128]], 1, ALU.is_ge)
    K.op(POOL, lambda e: e.memset(ltb[64:128, 0:64], 0.0), writes=[CONST])
    for h in range(H):
        K.op(POOL, lambda e: e.tensor_copy(mskf[:, h, :], ltf[:]), writes=[CONST])
        K.op(POOL, lambda e: e.tensor_copy(mskb[:, h, :], ltb[:]), writes=[CONST])
        aff(mskb[:, h, :], [[-1, 128]], 1, ALU.is_gt)
    K.op(POOL, lambda e: e.memset(sel2[:], 1.0), writes=[CONST])
    aff(sel2[:, 0, :], [[0, 128]], 1, ALU.is_equal)
    aff(sel2[:, 1, :], [[0, 128]], 1, ALU.is_equal, base=-1)
    K.op(POOL, lambda e: e.memset(ones_bf[:], 1.0), writes=[CONST])
    K.op(POOL, lambda e: e.memset(ones_f[:], 1.0), writes=[CONST])

    colv = sb("colv", [128, 4, KC, 2])
    gtbc = [[sb(f"gtbc{m}{g}", [128, D]) for g in range(2)] for m in range(2)]
    scT = sb("scT", [128, KC, 2])
    gmix = sb("gmix", [128, DEPTH, KC])
    gffn = sb("gffn", [128, DEPTH, KC])

    def rows_to_cols(dst_t, dst_ap, row_ap_fn, n, reads):
        bk = bank()
        for i in range(n):
            K.op(PE, lambda e: e.matmul(bk[:, i:i + 1], row_ap_fn(i), ident[0:1, 0:1], start=True, stop=True),
                 reads=reads + [CONST], writes=[bk.b], inc=(i == n - 1))
        K.op(DVE, lambda e: e.tensor_copy(dst_ap, bk[:, 0:n]), reads=[bk.b], writes=[dst_t.b])

    with ExitStack() as st:
        crow = sb("crow", [1, 2, D], stack=st)
        grow = sb("grow", [1, 2 * DEPTH, D], stack=st)
        K.dma(SP, crow[0:1, 0, :], c_in[0:1, :], writes=[crow.b])
        K.dma(SP, crow[0:1, 1, :], cctx_in[0:1, :], writes=[crow.b])
        K.dma(SP, grow[0:1, 0:DEPTH, :], norm_mix_g[None, :, :], writes=[grow.b])
        K.dma(SP, grow[0:1, DEPTH:2 * DEPTH, :], norm_ffn_g[None, :, :], writes=[grow.b])
        ctmp = sb("ctmp", [128, KC, 2], stack=st)
        bk = bank()
        for k in range(KC):
            for m in range(2):
                K.op(PE, lambda e: e.matmul(bk[:, 2 * k + m:2 * k + m + 1], crow[0:1, m, k * 128:(k + 1) * 128],
                                            ident[0:1, 0:1], start=True, stop=True),
                     reads=[crow.b, CONST], writes=[bk.b], inc=(k == KC - 1 and m == 1))
        K.op(ACT, lambda e: e.activation(out=scT[:].rearrange("p k m -> p (k m)"), in_=bk[:, 0:2 * KC], func=AF.Silu),
             reads=[bk.b], writes=[scT.b])
        for (dst, base) in ((gmix, 0), (gffn, DEPTH)):
            bk = bank()
            for l in range(DEPTH):
                for k in range(KC):
                    K.op(PE, lambda e: e.matmul(bk[:, l * KC + k:l * KC + k + 1], grow[0:1, base + l, k * 128:(k + 1) * 128],
                                                ident[0:1, 0:1], start=True, stop=True),
                         reads=[grow.b, CONST], writes=[bk.b], inc=(l == DEPTH - 1 and k == KC - 1))
            K.op(DVE, lambda e: e.tensor_copy(dst[:].rearrange("p l k -> p (l k)"), bk[:, 0:DEPTH * KC]),
                 reads=[bk.b], writes=[dst.b])
        K.barrier()

    colb_all = [sb(f"colb{q}", [128, 5, 2 * KC]) for q in range(2)]
    wdw_all = [sb(f"wdw{q}", [128, KC, CW]) for q in range(2)]
    for jj in range(nlayers // 2):
        with ExitStack() as st1:
            prow = sb("c_prow", [1, 5, 2 * D], stack=st1)
            K.dma(SP, prow[0:1, 0, :], conv_b_pw1[jj:jj + 1, :], writes=[prow.b])
            K.dma(SP, prow[0:1, 1, 0:D], conv_b_dw[jj:jj + 1, :], writes=[prow.b])
            K.dma(SP, prow[0:1, 2, 0:D], conv_ln_g[jj:jj + 1, :], writes=[prow.b])
            K.dma(SP, prow[0:1, 3, 0:D], conv_ln_b[jj:jj + 1, :], writes=[prow.b])
            for v in range(4):
                nn = 16 if v == 0 else 8
                rows_to_cols(colb_all[jj], colb_all[jj][:, v, 0:nn], lambda q, v=v: prow[0:1, v, q * 128:(q + 1) * 128], nn, [prow.b])
            wrow = sb("c_wrow", [1, CW, D], stack=st1)
            K.dma(SP, wrow[0:1, :, :], conv_w_dw[jj:jj + 1, :, :], writes=[wrow.b])
            for c in range(KC):
                rows_to_cols(wdw_all[jj], wdw_all[jj][:, c, :], lambda q, c=c: wrow[0:1, q, c * 128:(c + 1) * 128], CW, [wrow.b])
            K.barrier()

    def mod_phase(i):
        with ExitStack() as st:
            mod2 = sb("mod2", [2, 6 * D], stack=st)
            bm2 = sb("bm2", [2, 6 * D], stack=st)
            wm = [sb(f"wm{q}", [128, KC, 512], stack=st) for q in range(2)]
            K.dma(SP, bm2[0:1, :], b_mod[i:i + 1, :], writes=[bm2.b])
            K.dma(SP, bm2[1:2, :], b_mod[i:i + 1, :], writes=[bm2.b])
            wv = w_mod[i].rearrange("(k p) n -> p k n", p=128)
            for cb in range(12):
                w = wm[cb % 2]
                K.dma(SP, w[:], wv[:, :, cb * 512:(cb + 1) * 512], writes=[w.b])
                bk = bank()
                for k in range(KC):
                    K.op(PE, lambda e: e.matmul(bk[0:2, :], scT[:, k, :], w[:, k, :], start=(k == 0), stop=(k == KC - 1)),
                         reads=[scT.b, w.b], writes=[bk.b], inc=(k == KC - 1))
                K.op(DVE, lambda e: e.tensor_tensor(out=mod2[:, cb * 512:(cb + 1) * 512], in0=bk[0:2, :],
                                                    in1=bm2[:, cb * 512:(cb + 1) * 512], op=ALU.add),
                     reads=[bk.b, bm2.b], writes=[mod2.b])
            bk = bank()
            segs = (0, 1, 3, 4)
            n = 0
            for vi, seg in enumerate(segs):
                for k in range(KC):
                    col = (vi * KC + k) * 2
                    K.op(PE, lambda e: e.matmul(bk[:, col:col + 2], mod2[0:2, seg * D + k * 128: seg * D + (k + 1) * 128],
                                                ident[0:2, 0:2], start=True, stop=True),
                         reads=[mod2.b, CONST], writes=[bk.b], inc=(vi == 3 and k == KC - 1))
            K.op(DVE, lambda e: e.tensor_copy(colv[:].rearrange("p v k m -> p (v k m)"), bk[:, 0:4 * KC * 2]),
                 reads=[bk.b], writes=[colv.b])
            for (vi, g) in ((1, gmix), (3, gffn)):
                for m in range(2):
                    K.op(DVE, lambda e: e.scalar_tensor_tensor(out=colv[:, vi, :, m], in0=colv[:, vi, :, m], scalar=1.0,
                                                               in1=g[:, i, :], op0=ALU.add, op1=ALU.mult),
                         reads=[colv.b, g.b], writes=[colv.b])
            for m in range(2):
                for gi, seg in enumerate((2, 5)):
                    for hf in range(2):
                        bk = bank()
                        K.op(PE, lambda e: e.matmul(bk[:, :], sel2[0:2, m, :], mod2[0:2, seg * D + hf * 512: seg * D + (hf + 1) * 512],
                                                    start=True, stop=True), reads=[mod2.b, CONST], writes=[bk.b])
                        K.op(ACT, lambda e: e.copy(gtbc[m][gi][:, hf * 512:(hf + 1) * 512], bk[:, :]),
                             reads=[bk.b], writes=[gtbc[m][gi].b])
            K.barrier()

    def tile_rows(src, st, col_major):
        if st < 2:
            return [(0, 128, src[st * 128:(st + 1) * 128, :])]
        t = st - 2
        if not col_major:
            return [(0, 128, src[t * 128:(t + 1) * 128, :])]
        v = src[0:L, :].rearrange("(r c) d -> c r d", c=64)
        return [(0, 64, v[2 * t]), (64, 64, v[2 * t + 1])]

    def xsrc(i, st):
        cm = (i // 2) % 2 == 1
        if i == 0:
            if st < 2:
                return [(0, 128, ctx_in[st * 128:(st + 1) * 128, :])]
            return tile_rows(x_in, st, cm)
        if st < 2:
            return [(0, 128, XD[L + st * 128: L + (st + 1) * 128, :])]
        return tile_rows(XD, st, cm)

    def xdst(i, st):
        cm = (i // 2) % 2 == 1
        if st < 2:
            return [(0, 128, XD[L + st * 128: L + (st + 1) * 128, :])]
        return tile_rows(XD, st, cm)

    def load_tile(dst_t, dst_ap_fn, pieces):
        for (p0, n, ap) in pieces:
            K.dma(SP, dst_ap_fn(p0, n), ap, writes=[dst_t.b])

    def store_tile(src_t, src_ap_fn, pieces):
        for (p0, n, ap) in pieces:
            K.dma(POOL, ap, src_ap_fn(p0, n), reads=[src_t.b])

    _SENT = object()

    def interleave(main, fill, k=1):
        for _ in main:
            for _q in range(k):
                if fill is None or next(fill, _SENT) is _SENT:
                    fill = None
                    break
        if fill is not None:
            for _ in fill:
                pass

    def drain(gen):
        for _ in gen:
            pass

    def prep_gen(xg, ntl, mi, which, hT, scr, bankfn=None):
        bankfn = bankfn or bank
        ss, lnv, rstd, junk, xn = scr
        K.op(DVE, lambda e: e.memset(ss[:, 0:ntl], 0.0), writes=[ss.b])
        for tl in range(ntl):
            K.op(ACT, lambda e: e.activation(out=junk[:], in_=xg[:, tl, :], func=AF.Square, accum_out=ss[:, tl:tl + 1]),
                 reads=[xg.b], writes=[junk.b, ss.b])
        K.op(ACT, lambda e: e.activation(out=lnv[:, 0:ntl], in_=ss[:, 0:ntl], func=AF.Ln, scale=1.0 / D, bias=EPS),
             reads=[ss.b], writes=[lnv.b])
        K.op(ACT, lambda e: e.activation(out=rstd[:, 0:ntl], in_=lnv[:, 0:ntl], func=AF.Exp, scale=-0.5),
             reads=[lnv.b], writes=[rstd.b])
        for tl in range(min(ntl, 2)):
            x_n = xn[tl % 2]
            K.op(DVE, lambda e: e.tensor_scalar(out=x_n[:], in0=xg[:, tl, :], scalar1=rstd[:, tl:tl + 1], scalar2=None, op0=ALU.mult),
                 reads=[xg.b, rstd.b], writes=[x_n.b])
        yield
        yield
        yield
        for tl in range(ntl):
            x_n = xn[tl % 2]
            if tl >= 2:
                K.op(DVE, lambda e: e.tensor_scalar(out=x_n[:], in0=xg[:, tl, :], scalar1=rstd[:, tl:tl + 1], scalar2=None, op0=ALU.mult),
                     reads=[xg.b, rstd.b], writes=[x_n.b])
            for hf in range(2):
                bk = bankfn()
                for kk in range(4):
                    k = hf * 4 + kk
                    K.op(PE, lambda e: e.transpose(bk[:, kk * 128:(kk + 1) * 128], x_n[:, k * 128:(k + 1) * 128], ident[:]),
                         reads=[x_n.b, CONST], writes=[bk.b], inc=(kk == 3))
                for kk in range(4):
                    k = hf * 4 + kk
                    eng = ACT if kk % 2 == 0 else DVE
                    if eng is ACT:
                        K.op(ACT, lambda e: e.activation(out=hT[:, k, tl * 128:(tl + 1) * 128], in_=bk[:, kk * 128:(kk + 1) * 128],
                                                         func=AF.Identity, scale=colv[:, 2 * which + 1, k, mi:mi + 1],
                                                         bias=colv[:, 2 * which, k, mi:mi + 1]),
                             reads=[bk.b, colv.b], writes=[hT.b])
                    else:
                        K.op(DVE, lambda e: e.tensor_scalar(out=hT[:, k, tl * 128:(tl + 1) * 128], in0=bk[:, kk * 128:(kk + 1) * 128],
                                                            scalar1=colv[:, 2 * which + 1, k, mi:mi + 1],
                                                            scalar2=colv[:, 2 * which, k, mi:mi + 1], op0=ALU.mult, op1=ALU.add),
                             reads=[bk.b, colv.b], writes=[hT.b])
                yield

    def prep_group(xg, ntl, mi, which, hT, scr):
        drain(prep_gen(xg, ntl, mi, which, hT, scr))

    def prep_scratch(st):
        ss = sb("p_ss", [128, 8], stack=st)
        lnv = sb("p_ln", [128, 8], stack=st)
        rstd = sb("p_rstd", [128, 8], stack=st)
        junk = sb("p_junk", [128, D], BF16, stack=st)
        xn = [sb(f"p_xn{q}", [128, D], stack=st) for q in range(2)]
        return (ss, lnv, rstd, junk, xn)

    def residual(xg, tl, hf, yb, gt, tmp):
        sl = slice(hf * 512, (hf + 1) * 512)
        K.op(DVE, lambda e: e.tensor_tensor(out=tmp[:], in0=yb[:, :], in1=gt[:, sl], op=ALU.mult),
             reads=[yb.b, gt.b], writes=[tmp.b])
        K.op(POOL, lambda e: e.tensor_tensor(out=xg[:, tl, sl], in0=xg[:, tl, sl], in1=tmp[:], op=ALU.add),
             reads=[tmp.b, xg.b], writes=[xg.b])

    def ffn_phase(i, last):
        bg_cast(100)
        if i + 1 < nlayers:
            casts_for_layer(i + 1)
        with ExitStack() as st:
            woutb = sb("woutb", [128, 22, D], BF16, stack=st)
            wo_v = ffn_out_b[i].rearrange("(k p) n -> p k n", p=128)
            for q in range(2):
                K.dma(SP, woutb[:, q * 11:(q + 1) * 11, :], wo_v[:, q * 11:(q + 1) * 11, :], reads=WB[f"fout{i}"], writes=[woutb.b])
            pieces = [sb(f"wpiece{q}", [128, KC, 2, 512], BF16, stack=st) for q in range(2)]
            xgs = [sb(f"f_xg{q}", [128, 4, D], stack=st) for q in range(2)]
            hTs = [sb(f"f_hT{q}", [128, KC, 512], BF16, stack=st) for q in range(2)]
            act = sb("f_act", [128, 22, 512], BF16, stack=st)
            sa = [sb(f"f_sa{q}", [128, 512], stack=st) for q in range(2)]
            tmp = [sb(f"f_tmp{q}", [128, 512], stack=st) for q in range(2)]
            scr = prep_scratch(st)
            if last:
                fg = sb("f_fg", [128, D], stack=st)
                bkg = [bank(), bank()]
                frow = sb("f_frow", [1, D], stack=st)
                K.dma(SP, frow[0:1, :], final_norm_g[0:1, :], writes=[frow.b])
                for hf in range(2):
                    K.op(PE, lambda e: e.matmul(bkg[hf][:, :], ones_f[0:1, :], frow[0:1, hf * 512:(hf + 1) * 512], start=True, stop=True),
                         reads=[frow.b, CONST], writes=[bkg[hf].b])
                    K.op(ACT, lambda e: e.copy(fg[:, hf * 512:(hf + 1) * 512], bkg[hf][:, :]), reads=[bkg[hf].b], writes=[fg.b])
                fss = sb("f_fss", [128, 4], stack=st)
                fln = sb("f_fln", [128, 4], stack=st)
                frs = sb("f_frs", [128, 4], stack=st)
                fo = [sb(f"f_fo{q}", [128, D], stack=st) for q in range(2)]
                fjunk = scr[3]
            groups = [(g * 512, 4, 0) for g in range(8)]
            if not last:
                groups.append((L, 2, 1))
            src = XD

            def load_group(gi):
                r0, ntl, mi = groups[gi]
                xg = xgs[gi % 2]
                for tl in range(ntl):
                    K.dma(SP, xg[:, tl, :], src[r0 + tl * 128: r0 + (tl + 1) * 128, :], writes=[xg.b])

            npc = 6

            def load_piece(pc, pt):
                K.dma(SP, pt[:].rearrange("p k h w -> p (k h w)"), ffn_in_b[i, pc],
                      reads=WB[f"fin{i}"][2 * pc: 2 * pc + 2], writes=[pt.b])

            load_group(0)
            pcount = 0
            total_pieces = npc * len(groups)
            issued = [0]

            def issue_upto(n):
                while issued[0] <= min(n, total_pieces - 1):
                    load_piece(issued[0] % npc, pieces[issued[0] % 2])
                    issued[0] += 1

            issue_upto(1)
            prep_group(xgs[0], groups[0][1], groups[0][2], 1, hTs[0], scr)
            for gi in range(len(groups)):
                r0, ntl, mi = groups[gi]
                ntok = ntl * 128
                xg = xgs[gi % 2]
                hT = hTs[gi % 2]
                for pc in range(npc):
                    pt = pieces[pcount % 2]
                    issue_upto(pcount + 1)
                    pcount += 1
                    if pc == 2 and gi + 1 < len(groups):
                        load_group(gi + 1)
                    if pc == 4 and gi + 1 < len(groups):
                        fill_next = prep_gen(xgs[(gi + 1) % 2], groups[gi + 1][1], groups[gi + 1][2], 1, hTs[(gi + 1) % 2], scr)
                        next(fill_next, None)
                    nj = 4 if pc < 5 else 2
                    for jj in range(nj):
                        j = pc * 4 + jj
                        ba, bb = bank(), bank()
                        for hh, bk in ((0, ba), (1, bb)):
                            for k in range(KC):
                                K.op(PE, lambda e: e.matmul(bk[:, 0:ntok], pt[:, k, hh, jj * 128:(jj + 1) * 128], hT[:, k, 0:ntok],
                                                            start=(k == 0), stop=(k == KC - 1)),
                                     reads=[pt.b, hT.b], writes=[bk.b], inc=(k == KC - 1))
                        s_ = sa[j % 2]
                        K.op(ACT, lambda e: e.activation(out=s_[:, 0:ntok], in_=ba[:, 0:ntok], func=AF.Silu),
                             reads=[ba.b], writes=[s_.b])
                        K.op(DVE, lambda e: e.tensor_tensor(out=act[:, j, 0:ntok], in0=bb[:, 0:ntok], in1=s_[:, 0:ntok], op=ALU.mult),
                             reads=[bb.b, s_.b], writes=[act.b])

                def wout_gen():
                    for tl in range(ntl):
                        for hf in range(2):
                            yb = bank()
                            for k in range(22):
                                K.op(PE, lambda e: e.matmul(yb[:, :], act[:, k, tl * 128:(tl + 1) * 128], woutb[:, k, hf * 512:(hf + 1) * 512],
                                                            start=(k == 0), stop=(k == 21)),
                                     reads=[act.b, woutb.b], writes=[yb.b], inc=(k == 21))
                            residual(xg, tl, hf, yb, gtbc[mi][1], tmp[hf])
                            yield

                issue_upto(pcount + 1)
                fill = fill_next if gi + 1 < len(groups) else None
                interleave(wout_gen(), fill, k=2)
                bg_cast(3)
                if not last:
                    for tl in range(ntl):
                        K.dma(POOL, XD[r0 + tl * 128: r0 + (tl + 1) * 128, :], xg[:, tl, :], reads=[xg.b])
                else:
                    K.op(DVE, lambda e: e.memset(fss[:, 0:ntl], 0.0), writes=[fss.b])
                    for tl in range(ntl):
                        K.op(ACT, lambda e: e.activation(out=fjunk[:], in_=xg[:, tl, :], func=AF.Square, accum_out=fss[:, tl:tl + 1]),
                             reads=[xg.b], writes=[fjunk.b, fss.b])
                    K.op(ACT, lambda e: e.activation(out=fln[:, 0:ntl], in_=fss[:, 0:ntl], func=AF.Ln, scale=1.0 / D, bias=EPS),
                         reads=[fss.b], writes=[fln.b])
                    K.op(ACT, lambda e: e.activation(out=frs[:, 0:ntl], in_=fln[:, 0:ntl], func=AF.Exp, scale=-0.5),
                         reads=[fln.b], writes=[frs.b])
                    for tl in range(ntl):
                        o_ = fo[tl % 2]
                        K.op(DVE, lambda e: e.scalar_tensor_tensor(out=o_[:], in0=xg[:, tl, :], scalar=frs[:, tl:tl + 1], in1=fg[:],
                                                                   op0=ALU.mult, op1=ALU.mult),
                             reads=[xg.b, frs.b, fg.b], writes=[o_.b])
                        K.dma(POOL, out[r0 + tl * 128: r0 + (tl + 1) * 128, :], o_[:], reads=[o_.b])
            K.barrier()

    def gla_front(dirn, qT, kT, ktm, sp_, S, n):
        lt = ltf if dirn == 0 else ltb
        msk = mskf if dirn == 0 else mskb
        E2, E1T, E2T, ke, qeA, qeB, keT, ATm = S["work"][n % 2]
        sp_ap, sp_b = sp_
        fb = banks[6]
        v3 = lambda t_: t_[:].rearrange("p (h t) -> p h t", h=H)
        K.op(PE, lambda e: e.matmul(fb[:, :], lt[:], sp_ap, start=True, stop=True), reads=[CONST, sp_b], writes=[fb.b])
        K.op(ACT, lambda e: e.activation(out=E2[:], in_=fb[:, :], func=AF.Exp, scale=1.0 / 16), reads=[fb.b], writes=[E2.b])
        K.op(DVE, lambda e: e.tensor_tensor(out=ke[:], in0=ktm[0], in1=E2[:], op=ALU.mult), reads=[ktm[1], E2.b], writes=[ke.b])
        yield
        for h in range(H):
            K.op(PE, lambda e: e.matmul(fb[:, h * 128:(h + 1) * 128], sp_ap[:, h * 128:(h + 1) * 128], lt[:], start=True, stop=True),
                 reads=[CONST, sp_b], writes=[fb.b], inc=(h == H - 1))
        K.op(ACT, lambda e: e.activation(out=E1T[:], in_=fb[:, :], func=AF.Exp, scale=-1.0 / 16), reads=[fb.b], writes=[E1T.b])
        K.op(ACT, lambda e: e.activation(out=E2T[:], in_=fb[:, :], func=AF.Exp, scale=1.0 / 16), reads=[fb.b], writes=[E2T.b])
        yield
        for c, qe in ((0, qeA), (1, qeB)):
            K.op(DVE, lambda e: e.scalar_tensor_tensor(out=v3(qe)[:, :, c * 64:(c + 1) * 64], in0=qT[0](c * 64, (c + 1) * 64), scalar=float(DK) ** -0.5,
                                                       in1=v3(E1T)[:, :, c * 64:(c + 1) * 64], op0=ALU.mult, op1=ALU.mult),
                 reads=[qT[1], E1T.b], writes=[qe.b])
        K.op(DVE, lambda e: e.tensor_tensor(out=v3(keT), in0=kT[0](0, 128), in1=v3(E2T), op=ALU.mult),
             reads=[kT[1], E2T.b], writes=[keT.b])
        yield
        yield
        for h in range(H):
            hs = slice(h * 128, (h + 1) * 128)
            K.op(PE, lambda e: e.matmul(fb[:, h * 128: h * 128 + 64], keT[:, hs], qeA[:, h * 128: h * 128 + 64], start=True, stop=True),
                 reads=[keT.b, qeA.b], writes=[fb.b], inc=False)
            K.op(PE, lambda e: e.matmul(fb[:, h * 128 + 64: h * 128 + 128], keT[:, hs], qeB[:, h * 128 + 64: h * 128 + 128], start=True, stop=True),
                 reads=[keT.b, qeB.b], writes=[fb.b], inc=(h == H - 1))
        K.op(DVE, lambda e: e.tensor_tensor(out=ATm[:], in0=fb[:, :], in1=msk[:].rearrange("p h t -> p (h t)"), op=ALU.mult),
             reads=[fb.b, CONST], writes=[ATm.b])
        yield

    def gla_chunks(dirn, vbf, S, n, obanks):
        E2, E1T, E2T, ke, qeA, qeB, keT, ATm = S["work"][n % 2]
        Tst, Sbf, dec = S["T"], S["Sbf"], S["dec"]
        for h in range(H):
            ob = obanks[h]
            K.op(PE, lambda e: e.matmul(ob[:, 0:256], ATm[:, h * 128:(h + 1) * 128], vbf[:, h * 256:(h + 1) * 256], start=True, stop=False),
                 reads=[ATm.b, vbf.b], writes=[ob.b], inc=(h == H - 1))
        chunks = (0, 1) if dirn == 0 else (1, 0)
        ub = banks[4]
        for ci, c in enumerate(chunks):
            cs = slice(c * 64, (c + 1) * 64)
            qe = qeA if c == 0 else qeB
            lastcol = (c * 64 + 63) if dirn == 0 else (c * 64)
            dprev = dec[S["d"] % 2]
            dnew = dec[(S["d"] + 1) % 2]
            S["d"] += 1
            for h in range(H):
                K.op(ACT, lambda e: e.activation(out=Sbf[:, h, :], in_=Tst[:, h, :], func=AF.Identity, scale=dprev[:, h:h + 1]),
                     reads=[Tst.b, dprev.b], writes=[Sbf.b])
            K.op(POOL, lambda e: e.tensor_copy(dnew[:, :], E1T[:].rearrange("p (h t) -> p h t", h=H)[:, :, lastcol]),
                 reads=[E1T.b], writes=[dnew.b])
            yield
            for pr in range(2):
                for h in (2 * pr, 2 * pr + 1):
                    K.op(PE, lambda e: e.matmul(ub[:, (h % 2) * 256:(h % 2 + 1) * 256], ke[cs, h * 128:(h + 1) * 128], vbf[cs, h * 256:(h + 1) * 256],
                                                start=True, stop=True), reads=[ke.b, vbf.b], writes=[ub.b], inc=(h % 2 == 1))
                for h in (2 * pr, 2 * pr + 1):
                    ob = obanks[h]
                    K.op(PE, lambda e: e.matmul(ob[:, 0:256], qe[:, h * 128:(h + 1) * 128], Sbf[:, h, :], start=False, stop=(ci == 1)),
                         reads=[qe.b, Sbf.b], writes=[ob.b], inc=True)
                for h in (2 * pr, 2 * pr + 1):
                    K.op(DVE, lambda e: e.scalar_tensor_tensor(out=Tst[:, h, :], in0=Tst[:, h, :], scalar=dprev[:, h:h + 1],
                                                               in1=ub[:, (h % 2) * 256:(h % 2 + 1) * 256], op0=ALU.mult, op1=ALU.add),
                         reads=[Tst.b, dprev.b, ub.b], writes=[Tst.b])
                yield

    def gla_state(st, tag):
        work = []
        for q in range(2):
            work.append((sb(f"{tag}E2{q}", [128, 512], stack=st), sb(f"{tag}E1T{q}", [128, 512], stack=st),
                         sb(f"{tag}E2T{q}", [128, 512], stack=st), sb(f"{tag}ke{q}", [128, 512], BF16, stack=st),
                         sb(f"{tag}qeA{q}", [128, 512], BF16, stack=st), sb(f"{tag}qeB{q}", [128, 512], BF16, stack=st),
                         sb(f"{tag}keT{q}", [128, 512], BF16, stack=st),
                         sb(f"{tag}ATm{q}", [128, 512], BF16, stack=st)))
        S = {"work": work, "n": 0, "d": 0,
             "T": sb(f"{tag}T", [128, H, DV], stack=st), "Sbf": sb(f"{tag}Sbf", [128, H, DV], BF16, stack=st),
             "dec": [sb(f"{tag}dec{q}", [128, H], stack=st) for q in range(2)]}
        for q in range(2):
            K.op(POOL, lambda e: e.memset(work[q][4][:], 0.0), writes=[work[q][4].b])
            K.op(POOL, lambda e: e.memset(work[q][5][:], 0.0), writes=[work[q][5].b])
        K.op(POOL, lambda e: e.memset(S["T"][:], 0.0), writes=[S["T"].b])
        K.op(POOL, lambda e: e.memset(S["dec"][0][:], 1.0), writes=[S["dec"][0].b])
        K.op(POOL, lambda e: e.memset(S["dec"][1][:], 1.0), writes=[S["dec"][1].b])
        return S

    def softplus_neg(dst_ap, dst_b, pre_bank, etmp):
        K.op(ACT, lambda e: e.activation(out=etmp[:], in_=pre_bank[:, :], func=AF.Exp, scale=-1.0), reads=[pre_bank.b], writes=[etmp.b])
        K.op(ACT, lambda e: e.activation(out=dst_ap, in_=etmp[:], func=AF.Ln, bias=1.0), reads=[etmp.b], writes=[dst_b])

    def run_pipeline3(P, F_of, C_of, ntile, psteps=3):
        state = {"done": -1, "alive": P is not None}

        def advance(upto, maxsteps):
            steps = 0
            while state["alive"] and state["done"] < upto and (maxsteps is None or steps < maxsteps):
                r = next(P, _SENT)
                if r is _SENT:
                    state["alive"] = False
                    break
                if r is not None:
                    state["done"] = r[1]
                steps += 1

        if P is not None:
            advance(min(1, ntile - 1), None)
        drain(F_of(0))
        for n in range(ntile):
            f = F_of(n + 1) if n + 1 < ntile else None
            for _ in C_of(n):
                if f is not None and next(f, _SENT) is _SENT:
                    f = None
                if P is not None:
                    advance(min(n + 2, ntile - 1), psteps)
            if f is not None:
                drain(f)
            if P is not None:
                advance(min(n + 2, ntile - 1), None)
        if P is not None:
            for _ in P:
                pass

    def gla_phaseA(i):
        j = i // 2
        pbank_rr = [0]

        def pbank():
            b = banks[(5, 7)[pbank_rr[0] % 2]]
            pbank_rr[0] += 1
            return b

        with ExitStack() as st:
            winb = sb("a_win", [128, KC, GIN], BF16, stack=st)
            wv = gla_w_in_b[j].rearrange("(k p) n -> p k n", p=128)
            for q in range(4):
                K.dma(SP, winb[:, 2 * q:2 * q + 2, :], wv[:, 2 * q:2 * q + 2, :], reads=WB[f"gin{j}"], writes=[winb.b])
            wstage = sb("a_wst", [17, 2, 512], stack=st)
            wab = sb("a_wab", [17, 2, 512], BF16, stack=st)
            for dd, (wa, ba) in enumerate(((gla_wa_f, gla_ba_f), (gla_wa_b, gla_ba_b))):
                K.dma(SP, wstage[0:16, dd, :], wa[j], writes=[wstage.b])
                K.dma(SP, wstage[16:17, dd, :], ba[j:j + 1, :], writes=[wstage.b])
            K.op(DVE, lambda e: e.tensor_copy(wab[:], wstage[:]), reads=[wstage.b], writes=[wab.b])
            xgs = [sb(f"a_xg{q}", [128, 2, D], stack=st) for q in range(2)]
            hT = sb("a_hT", [128, KC, 256], BF16, stack=st)
            qkT = [sb(f"a_qkT{q}", [128, 8, 256], stack=st) for q in range(2)]
            stg = [sb(f"a_stg{q}", [128, 3072], stack=st) for q in range(3)]
            vbf = [sb(f"a_v{q}", [128, D], BF16, stack=st) for q in range(3)]
            spf = [sb(f"a_spf{q}", [128, 512], stack=st) for q in range(3)]
            etmp = sb("a_etmp", [128, 512], stack=st)
            gT = [sb(f"a_gT{q}", [17, 2, 128], BF16, stack=st) for q in range(3)]
            for q in range(3):
                K.op(POOL, lambda e: e.memset(gT[q][:], 1.0), writes=[gT[q].b])
            scr = prep_scratch(st)
            S = gla_state(st, "a_")
            groups = [([0, 1], 1)] + [([2 + g * 2 + t for t in range(2)], 0) for g in range(16)]
            seq = [(gi, tl, stile) for gi, (tiles, mi) in enumerate(groups) for tl, stile in enumerate(tiles)]

            def load_group(gi):
                tiles, mi = groups[gi]
                xg = xgs[gi % 2]
                for tl, stile in enumerate(tiles):
                    load_tile(xg, lambda p0, n, tl=tl: xg[p0:p0 + n, tl, :], xsrc(i, stile))

            def P_stream():
                n = 0
                load_group(0)
                for gi, (tiles, mi) in enumerate(groups):
                    ntl = len(tiles)
                    ntok = ntl * 128
                    xg = xgs[gi % 2]
                    if gi + 1 < len(groups):
                        load_group(gi + 1)
                    yield from prep_gen(xg, ntl, mi, 0, hT, scr, bankfn=pbank)
                    qk = qkT[gi % 2]
                    for mc in range(8):
                        bk = pbank()
                        for k in range(KC):
                            K.op(PE, lambda e: e.matmul(bk[:, 0:ntok], winb[:, k, mc * 128:(mc + 1) * 128], hT[:, k, 0:ntok],
                                                        start=(k == 0), stop=(k == KC - 1)), reads=[winb.b, hT.b], writes=[bk.b], inc=(k == KC - 1))
                        if mc % 2 == 0:
                            K.op(ACT, lambda e: e.copy(qk[:, mc, 0:ntok], bk[:, 0:ntok]), reads=[bk.b], writes=[qk.b])
                        else:
                            K.op(DVE, lambda e: e.tensor_copy(qk[:, mc, 0:ntok], bk[:, 0:ntok]), reads=[bk.b], writes=[qk.b])
                        yield
                    pos0 = tiles[0] * 128
                    K.dma(POOL, SP_FM.rearrange("e d t -> d e t")[:, :, pos0:pos0 + ntok], qk[:, :, 0:ntok], reads=[qk.b])
                    for tl, stile in enumerate(tiles):
                        sg = stg[n % 3]
                        v_ = vbf[n % 3]
                        sf = spf[n % 3]
                        g_ = gT[n % 3]
                        ts = slice(tl * 128, (tl + 1) * 128)
                        for cb in range(5):
                            bk = pbank()
                            c0 = 512 + cb * 512
                            for k in range(KC):
                                K.op(PE, lambda e: e.matmul(bk[:, :], hT[:, k, ts], winb[:, k, c0:c0 + 512], start=(k == 0), stop=(k == KC - 1)),
                                     reads=[winb.b, hT.b], writes=[bk.b], inc=(k == KC - 1))
                            if cb == 0:
                                K.op(ACT, lambda e: e.copy(sg[:, 0:512], bk[:, :]), reads=[bk.b], writes=[sg.b])
                            elif cb in (1, 2):
                                K.op(DVE, lambda e: e.tensor_copy(v_[:, (cb - 1) * 512: cb * 512], bk[:, :]), reads=[bk.b], writes=[v_.b])
                            else:
                                K.op(ACT, lambda e: e.copy(sg[:, 512 + (cb - 3) * 512: 512 + (cb - 2) * 512], bk[:, :]), reads=[bk.b], writes=[sg.b])
                            yield
                        bk = pbank()
                        for dd in range(2):
                            c0 = 3072 + dd * 16
                            for k in range(KC):
                                K.op(PE, lambda e: e.matmul(bk[0:16, dd * 128:(dd + 1) * 128], winb[:, k, c0:c0 + 16], hT[:, k, ts],
                                                            start=(k == 0), stop=(k == KC - 1)), reads=[winb.b, hT.b], writes=[bk.b],
                                     inc=(k == KC - 1 and dd == 1))
                        K.op(DVE, lambda e: e.tensor_copy(g_[0:16, :, :].rearrange("p a t -> p (a t)"), bk[0:16, 0:256]), reads=[bk.b], writes=[g_.b])
                        yield
                        for dd in range(2):
                            pb = pbank()
                            K.op(PE, lambda e: e.matmul(pb[:, :], g_[0:17, dd, :], wab[0:17, dd, :], start=True, stop=True),
                                 reads=[g_.b, wab.b], writes=[pb.b])
                            if dd == 0:
                                softplus_neg(sf[:], sf.b, pb, etmp)
                            else:
                                softplus_neg(sg[:, 1536:2048], sg.b, pb, etmp)
                            yield
                        yield ("tile", n)
                        n += 1

            def F_of(n):
                gi, tl, stile = seq[n]
                qk = qkT[gi % 2]
                sg = stg[n % 3]
                sf = spf[n % 3]
                yield from gla_front(0, (lambda c0, c1: qk[:, 0:4, tl * 128 + c0: tl * 128 + c1], qk.b),
                                     (lambda c0, c1: qk[:, 4:8, tl * 128 + c0: tl * 128 + c1], qk.b),
                                     (sg[:, 0:512], sg.b), (sf[:], sf.b), S, n)

            def C_of(n):
                gi, tl, stile = seq[n]
                sg = stg[n % 3]
                v_ = vbf[n % 3]
                ob = [banks[0], banks[1], banks[2], banks[3]]
                yield from gla_chunks(0, v_, S, n, ob)
                for h in range(H):
                    if h % 2 == 0:
                        K.op(ACT, lambda e: e.copy(sg[:, 2048 + h * 256: 2048 + (h + 1) * 256], ob[h][:, 0:256]), reads=[ob[h].b], writes=[sg.b])
                    else:
                        K.op(DVE, lambda e: e.tensor_copy(sg[:, 2048 + h * 256: 2048 + (h + 1) * 256], ob[h][:, 0:256]), reads=[ob[h].b], writes=[sg.b])
                pos = stile * 128
                K.dma(POOL, SP_TM[pos:pos + 128, :], sg[:], reads=[sg.b])
                K.dma(POOL, SP_V[pos:pos + 128, :], v_[:], reads=[v_.b])
                bg_cast(1)
                yield

            run_pipeline3(P_stream(), F_of, C_of, len(seq))
            K.barrier()

    def gla_phaseB(i):
        j = i // 2
        obank_rr = [0]

        def obank():
            b = banks[(5, 7)[obank_rr[0] % 2]]
            obank_rr[0] += 1
            return b

        with ExitStack() as st:
            woutb = sb("b_wout", [128, KC, D], BF16, stack=st)
            K.dma(SP, woutb[:], gla_w_out_b[j].rearrange("(k p) n -> p k n", p=128), reads=WB[f"gout{j}"], writes=[woutb.b])
            gnb = sb("b_gnb", [128, D], stack=st)
            grow = sb("b_grow", [1, D], stack=st)
            K.dma(SP, grow[0:1, :], gla_norm_g[j:j + 1, :], writes=[grow.b])
            for hf in range(2):
                bk = obank()
                K.op(PE, lambda e: e.matmul(bk[:, :], ones_f[0:1, :], grow[0:1, hf * 512:(hf + 1) * 512], start=True, stop=True),
                     reads=[grow.b, CONST], writes=[bk.b])
                K.op(ACT, lambda e: e.copy(gnb[:, hf * 512:(hf + 1) * 512], bk[:, :]), reads=[bk.b], writes=[gnb.b])
            qkT = [sb(f"b_qkT{q}", [128, 8, 256], stack=st) for q in range(2)]
            NB_ = 4
            stg = [sb(f"b_stg{q}", [128, 3072], stack=st) for q in range(NB_)]
            vbf = [sb(f"b_v{q}", [128, D], BF16, stack=st) for q in range(NB_)]
            xt = [sb(f"b_x{q}", [128, 1, D], stack=st) for q in range(3)]
            osum = [sb(f"b_os{q}", [128, D], stack=st) for q in range(2)]
            gr = [sb(f"b_gr{q}", [128, D], stack=st) for q in range(2)]
            og = [sb(f"b_og{q}", [128, D], stack=st) for q in range(2)]
            ogT = [sb(f"b_ogT{q}", [128, KC, 128], BF16, stack=st) for q in range(2)]
            hss = [sb(f"b_hss{q}", [128, 3, H], stack=st) for q in range(2)]
            junk = sb("b_junk", [128, DV], BF16, stack=st)
            tmp = [sb(f"b_tmp{q}", [128, 512], stack=st) for q in range(2)]
            S = gla_state(st, "b_")
            groups = [([1, 0], 1)] + [([2 + g * 2 + t for t in (1, 0)], 0) for g in range(15, -1, -1)]

            def load_group(gi):
                tiles, mi = groups[gi]
                lo = min(tiles) * 128
                n = len(tiles) * 128
                qk = qkT[gi % 2]
                K.dma(SP, qk[:, 0:4, 0:n], SP_FM.rearrange("e d t -> d e t")[:, 0:4, lo:lo + n], writes=[qk.b])
                K.dma(SP, qk[:, 4:8, 0:n], SP_FM.rearrange("e d t -> d e t")[:, 4:8, lo:lo + n], writes=[qk.b])

            seq = [(gi, stile) for gi, (tiles, mi) in enumerate(groups) for stile in tiles]

            def load_tile_b(n):
                gi, stile = seq[n]
                pos = stile * 128
                K.dma(SP, stg[n % NB_][:], SP_TM[pos:pos + 128, :], writes=[stg[n % NB_].b])
                K.dma(SP, vbf[n % NB_][:], SP_V[pos:pos + 128, :], writes=[vbf[n % NB_].b])

            def load_x(n):
                gi, stile = seq[n]
                load_tile(xt[n % 3], lambda p0, nn, n=n: xt[n % 3][p0:p0 + nn, 0, :], xsrc(i, stile))

            def F_of(n):
                gi, stile = seq[n]
                tiles, mi = groups[gi]
                tl = stile - min(tiles)
                qk = qkT[gi % 2]
                sg = stg[n % NB_]
                yield from gla_front(1, (lambda c0, c1: qk[:, 0:4, tl * 128 + c0: tl * 128 + c1], qk.b),
                                     (lambda c0, c1: qk[:, 4:8, tl * 128 + c0: tl * 128 + c1], qk.b),
                                     (sg[:, 0:512], sg.b), (sg[:, 1536:2048], sg.b), S, n)

            def core(n):
                sg = stg[n % NB_]
                ob = [banks[0], banks[1], banks[2], banks[3]]
                yield from gla_chunks(1, vbf[n % NB_], S, n, ob)
                os_ = osum[n % 2]
                for q in range(H):
                    K.op(DVE, lambda e: e.tensor_tensor(out=os_[:, q * 256:(q + 1) * 256], in0=ob[q][:, 0:256],
                                                        in1=sg[:, 2048 + q * 256: 2048 + (q + 1) * 256], op=ALU.add),
                         reads=[ob[q].b, sg.b], writes=[os_.b])
                yield

            def outp(n):
                gi, stile = seq[n]
                tiles, mi = groups[gi]
                sg = stg[n % NB_]
                os_ = osum[n % 2]
                hs_ = hss[n % 2]
                g_ = gr[n % 2]
                og_ = og[n % 2]
                ogT_ = ogT[n % 2]
                x_ = xt[n % 3]
                K.op(ACT, lambda e: e.activation(out=g_[:], in_=sg[:, 512:1536], func=AF.Silu), reads=[sg.b], writes=[g_.b])
                K.op(POOL, lambda e: e.tensor_tensor(out=g_[:], in0=g_[:], in1=gnb[:], op=ALU.mult), reads=[g_.b, gnb.b], writes=[g_.b])
                yield
                K.op(DVE, lambda e: e.memset(hs_[:, 0, :], 0.0), writes=[hs_.b])
                for h in range(H):
                    K.op(ACT, lambda e: e.activation(out=junk[:], in_=os_[:, h * DV:(h + 1) * DV], func=AF.Square, accum_out=hs_[:, 0, h:h + 1]),
                         reads=[os_.b], writes=[junk.b, hs_.b])
                K.op(ACT, lambda e: e.activation(out=hs_[:, 1, :], in_=hs_[:, 0, :], func=AF.Ln, scale=1.0 / DV, bias=EPS),
                     reads=[hs_.b], writes=[hs_.b])
                K.op(ACT, lambda e: e.activation(out=hs_[:, 2, :], in_=hs_[:, 1, :], func=AF.Exp, scale=-0.5), reads=[hs_.b], writes=[hs_.b])
                yield
                yield
                for h in range(H):
                    hs = slice(h * DV, (h + 1) * DV)
                    K.op(DVE, lambda e: e.scalar_tensor_tensor(out=og_[:, hs], in0=os_[:, hs], scalar=hs_[:, 2, h:h + 1], in1=g_[:, hs],
                                                               op0=ALU.mult, op1=ALU.mult), reads=[os_.b, hs_.b, g_.b], writes=[og_.b])
                yield
                yield
                for hf in range(2):
                    bk = obank()
                    for kk in range(4):
                        k = hf * 4 + kk
                        K.op(PE, lambda e: e.transpose(bk[:, kk * 128:(kk + 1) * 128], og_[:, k * 128:(k + 1) * 128], ident[:]),
                             reads=[og_.b, CONST], writes=[bk.b], inc=(kk == 3))
                    K.op(ACT, lambda e: e.copy(ogT_[:, hf * 4:(hf + 1) * 4, :].rearrange("p k t -> p (k t)"), bk[:, :]),
                         reads=[bk.b], writes=[ogT_.b])
                yield
                for hf in range(2):
                    yb = obank()
                    for k in range(KC):
                        K.op(PE, lambda e: e.matmul(yb[:, :], ogT_[:, k, :], woutb[:, k, hf * 512:(hf + 1) * 512], start=(k == 0), stop=(k == KC - 1)),
                             reads=[ogT_.b, woutb.b], writes=[yb.b], inc=(k == KC - 1))
                    residual(x_, 0, hf, yb, gtbc[mi][0], tmp[hf])
                    yield
                store_tile(x_, lambda p0, nn: x_[p0:p0 + nn, 0, :], xdst(i, stile))

            ntile = len(seq)
            load_group(0)
            load_tile_b(0)
            if ntile > 1:
                if seq[1][0] != seq[0][0]:
                    load_group(seq[1][0])
                load_tile_b(1)
            load_x(0)
            drain(F_of(0))
            for n in range(ntile):
                if n + 1 < ntile:
                    load_x(n + 1)
                if n + 2 < ntile:
                    if seq[n + 2][0] != seq[n + 1][0]:
                        load_group(seq[n + 2][0])
                    load_tile_b(n + 2)
                f = F_of(n + 1) if n + 1 < ntile else None
                o = outp(n - 1) if n > 0 else None
                for _ in core(n):
                    if f is not None and next(f, _SENT) is _SENT:
                        f = None
                    if o is not None and next(o, _SENT) is _SENT:
                        o = None
                if f is not None:
                    drain(f)
                if o is not None:
                    drain(o)
            drain(outp(ntile - 1))
            K.barrier()

    def conv_phase(i, last):
        j = i // 2
        colb = colb_all[j]
        wdw = wdw_all[j]
        with ExitStack() as st:
            groups = [([2 + g * 4 + t for t in range(4)], 0) for g in range(8)]
            if not last:
                groups = [([0, 1], 1)] + groups
            identb = sb("c_identb", [128, 128], BF16, stack=st)
            K.op(DVE, lambda e: e.tensor_copy(identb[:], ident[:]), reads=[CONST], writes=[identb.b])
            diag = sb("c_diag", [128, KC, CW, 128], BF16, stack=st)

            def diag_gen():
                for c in range(KC):
                    for tp in range(CW):
                        if tp % 2 == 0:
                            K.op(ACT, lambda e: e.activation(out=diag[:, c, tp, :], in_=identb[:], func=AF.Identity, scale=wdw[:, c, tp:tp + 1]),
                                 reads=[identb.b, wdw.b], writes=[diag.b])
                        else:
                            K.op(DVE, lambda e: e.tensor_scalar(out=diag[:, c, tp, :], in0=identb[:], scalar1=wdw[:, c, tp:tp + 1], scalar2=None, op0=ALU.mult),
                                 reads=[identb.b, wdw.b], writes=[diag.b])
                        if tp % 4 == 3:
                            yield

            with ExitStack() as st1:
                w1 = sb("c_w1", [128, KC, 2 * D], BF16, stack=st1)
                wv = pw1_b[j].rearrange("(k p) n -> p k n", p=128)
                for q in range(4):
                    K.dma(SP, w1[:, 2 * q:2 * q + 2, :], wv[:, 2 * q:2 * q + 2, :], reads=WB[f"pw1{j}"], writes=[w1.b])
                xgs = [sb(f"c_xg{q}", [128, 4, D], stack=st1) for q in range(2)]
                hTs = [sb(f"c_hT{q}", [128, KC, 512], BF16, stack=st1) for q in range(2)]
                ug = [sb(f"c_ug{q}", [128, KC, 512], BF16, stack=st1) for q in range(2)]
                sig = [sb(f"c_sig{q}", [128, 512], stack=st1) for q in range(2)]
                scr = prep_scratch(st1)
                dg = diag_gen()

                def load_group(gi):
                    tiles, mi = groups[gi]
                    xg = xgs[gi % 2]
                    for tl, stile in enumerate(tiles):
                        load_tile(xg, lambda p0, n, tl=tl: xg[p0:p0 + n, tl, :], xsrc(i, stile))

                load_group(0)
                if len(groups) > 1:
                    load_group(1)
                prep_group(xgs[0], len(groups[0][0]), groups[0][1], 0, hTs[0], scr)
                for gi, (tiles, mi) in enumerate(groups):
                    ntl = len(tiles)
                    ntok = ntl * 128
                    xg = xgs[gi % 2]
                    hT = hTs[gi % 2]
                    fillp = None
                    if gi + 1 < len(groups):
                        fillp = prep_gen(xgs[(gi + 1) % 2], len(groups[gi + 1][0]), groups[gi + 1][1], 0, hTs[(gi + 1) % 2], scr)
                    u_ = ug[gi % 2]
                    for c in range(KC):
                        if fillp is not None and c >= 1:
                            if next(fillp, _SENT) is _SENT:
                                fillp = None
                        ba, bb = bank(), bank()
                        for hh, bk in ((0, ba), (1, bb)):
                            for k in range(KC):
                                K.op(PE, lambda e: e.matmul(bk[:, 0:ntok], w1[:, k, hh * D + c * 128: hh * D + (c + 1) * 128], hT[:, k, 0:ntok],
                                                            start=(k == 0), stop=(k == KC - 1)), reads=[w1.b, hT.b], writes=[bk.b], inc=(k == KC - 1))
                        s_ = sig[c % 2]
                        K.op(ACT, lambda e: e.activation(out=s_[:, 0:ntok], in_=bb[:, 0:ntok], func=AF.Sigmoid, bias=colb[:, 0, KC + c:KC + c + 1]),
                             reads=[bb.b, colb.b], writes=[s_.b])
                        K.op(DVE, lambda e: e.scalar_tensor_tensor(out=u_[:, c, 0:ntok], in0=ba[:, 0:ntok], scalar=colb[:, 0, c:c + 1],
                                                                   in1=s_[:, 0:ntok], op0=ALU.add, op1=ALU.mult),
                             reads=[ba.b, colb.b, s_.b], writes=[u_.b])
                        next(dg, None)
                    if fillp is not None:
                        drain(fillp)
                    if gi + 2 < len(groups):
                        load_group(gi + 2)
                    pos0 = tiles[0] * 128
                    K.dma(POOL, UD.rearrange("c p t -> p c t")[:, :, pos0:pos0 + ntok], u_[:, :, 0:ntok], reads=[u_.b])
                drain(dg)
                K.barrier()
            with ExitStack() as st2:
                w2 = sb("c_w2", [128, KC, D], BF16, stack=st2)
                K.dma(SP, w2[:], pw2_b[j].rearrange("(k p) n -> p k n", p=128), reads=WB[f"pw2{j}"], writes=[w2.b])
                b2row = sb("c_b2row", [1, D], stack=st2)
                b2bf = sb("c_b2bf", [1, D], BF16, stack=st2)
                K.dma(SP, b2row[0:1, :], conv_b_pw2[j:j + 1, :], writes=[b2row.b])
                K.op(DVE, lambda e: e.tensor_copy(b2bf[:], b2row[:]), reads=[b2row.b], writes=[b2bf.b])
                ugs = [sb(f"c_uh{q}", [128, KC, 512 + 32], BF16, stack=st2) for q in range(2)]
                xg1 = sb("c_xh", [128, 4, D], stack=st2)
                Vs = [sb(f"c_V{q}", [128, KC, 512], stack=st2) for q in range(2)]
                vb = [sb(f"c_vb{q}", [128, 512], BF16, stack=st2) for q in range(2)]
                vq = [sb(f"c_vq{q}", [128, 512], BF16, stack=st2) for q in range(2)]
                mean = sb("c_mean", [128, 512], stack=st2)
                var = sb("c_var", [128, 512], stack=st2)
                rst = sb("c_rst", [128, 512], stack=st2)
                Sact = sb("c_S", [128, KC, 512], BF16, stack=st2)
                tmp = [sb(f"c_tmp{q}", [128, 512], stack=st2) for q in range(2)]
                HALO = CW // 2
                crr = [0]
                yrr = [0]

                def seg_bounds(mi):
                    return (0, CT) if mi == 1 else (CT, CT + L)

                def load_u(gi):
                    tiles, mi = groups[gi]
                    ntok = len(tiles) * 128
                    pos0 = tiles[0] * 128
                    lo_b, hi_b = seg_bounds(mi)
                    lo = max(lo_b, pos0 - HALO)
                    hi = min(hi_b, pos0 + ntok + HALO)
                    u_ = ugs[gi % 2]
                    K.dma(SP, u_[:, :, 16 + lo - pos0: 16 + hi - pos0], UD.rearrange("c p t -> p c t")[:, :, lo:hi], writes=[u_.b])

                def A_gen(gi):
                    tiles, mi = groups[gi]
                    ntok = len(tiles) * 128
                    pos0 = tiles[0] * 128
                    lo_b, hi_b = seg_bounds(mi)
                    u_ = ugs[gi % 2]
                    V = Vs[gi % 2]
                    bm, bq = banks[2 + 2 * (gi % 2)], banks[3 + 2 * (gi % 2)]
                    pend = []
                    for c in range(KC):
                        bk = banks[crr[0] % 2]
                        crr[0] += 1
                        order = [HALO] + [tp for tp in range(CW) if tp != HALO]
                        for n_, tp in enumerate(order):
                            sh = tp - HALO
                            t_lo = max(0, lo_b - (pos0 + sh))
                            t_hi = min(ntok, hi_b - (pos0 + sh))
                            K.op(PE, lambda e: e.matmul(bk[:, t_lo:t_hi], diag[:, c, tp, :], u_[:, c, 16 + t_lo + sh: 16 + t_hi + sh],
                                                        start=(n_ == 0), stop=(n_ == CW - 1)), reads=[diag.b, u_.b], writes=[bk.b], inc=(n_ == CW - 1))
                        while pend:
                            pend.pop(0)()
                        K.op(ACT, lambda e: e.activation(out=V[:, c, 0:ntok], in_=bk[:, 0:ntok], func=AF.Identity, bias=colb[:, 1, c:c + 1]),
                             reads=[bk.b, colb.b], writes=[V.b])
                        vb_, vq_ = vb[c % 2], vq[c % 2]
                        K.op(POOL, lambda e: e.tensor_copy(vb_[:, 0:ntok], V[:, c, 0:ntok]), reads=[V.b], writes=[vb_.b])
                        K.op(ACT, lambda e: e.activation(out=vq_[:, 0:ntok], in_=V[:, c, 0:ntok], func=AF.Square), reads=[V.b], writes=[vq_.b])

                        def stats(c=c, vb_=vb_, vq_=vq_):
                            K.op(PE, lambda e: e.matmul(bm[:, 0:ntok], ones_bf[:], vb_[:, 0:ntok], start=(c == 0), stop=(c == KC - 1)),
                                 reads=[vb_.b, CONST], writes=[bm.b], inc=True)
                            K.op(PE, lambda e: e.matmul(bq[:, 0:ntok], ones_bf[:], vq_[:, 0:ntok], start=(c == 0), stop=(c == KC - 1)),
                                 reads=[vq_.b, CONST], writes=[bq.b], inc=True)

                        pend.append(stats)
                        yield
                    while pend:
                        pend.pop(0)()
                    yield

                def B_gen(gi):
                    tiles, mi = groups[gi]
                    ntl = len(tiles)
                    ntok = ntl * 128
                    V = Vs[gi % 2]
                    bm, bq = banks[2 + 2 * (gi % 2)], banks[3 + 2 * (gi % 2)]
                    xg = xg1
                    for tl, stile in enumerate(tiles):
                        load_tile(xg, lambda p0, n, tl=tl: xg[p0:p0 + n, tl, :], xsrc(i, stile))
                    K.op(ACT, lambda e: e.activation(out=mean[:, 0:ntok], in_=bm[:, 0:ntok], func=AF.Identity, scale=1.0 / D), reads=[bm.b], writes=[mean.b])
                    K.op(DVE, lambda e: e.tensor_tensor(out=var[:, 0:ntok], in0=mean[:, 0:ntok], in1=mean[:, 0:ntok], op=ALU.mult),
                         reads=[mean.b], writes=[var.b])
                    K.op(DVE, lambda e: e.scalar_tensor_tensor(out=var[:, 0:ntok], in0=bq[:, 0:ntok], scalar=1.0 / D, in1=var[:, 0:ntok],
                                                               op0=ALU.mult, op1=ALU.subtract), reads=[bq.b, var.b], writes=[var.b])
                    K.op(ACT, lambda e: e.activation(out=rst[:, 0:ntok], in_=var[:, 0:ntok], func=AF.Ln, bias=EPS), reads=[var.b], writes=[rst.b])
                    K.op(ACT, lambda e: e.activation(out=rst[:, 0:ntok], in_=rst[:, 0:ntok], func=AF.Exp, scale=-0.5), reads=[rst.b], writes=[rst.b])
                    yield
                    for c in range(KC):
                        K.op(DVE, lambda e: e.tensor_tensor(out=V[:, c, 0:ntok], in0=V[:, c, 0:ntok], in1=mean[:, 0:ntok], op=ALU.subtract),
                             reads=[V.b, mean.b], writes=[V.b])
                        K.op(POOL, lambda e: e.tensor_tensor(out=V[:, c, 0:ntok], in0=V[:, c, 0:ntok], in1=rst[:, 0:ntok], op=ALU.mult),
                             reads=[V.b, rst.b], writes=[V.b])
                        K.op(ACT, lambda e: e.activation(out=Sact[:, c, 0:ntok], in_=V[:, c, 0:ntok], func=AF.Silu,
                                                         scale=colb[:, 2, c:c + 1], bias=colb[:, 3, c:c + 1]),
                             reads=[V.b, colb.b], writes=[Sact.b])
                        if c % 2 == 1:
                            yield
                    for _q in range(4):
                        yield
                    for tl, stile in enumerate(tiles):
                        for hf in range(2):
                            yb = banks[6 + yrr[0] % 2]
                            yrr[0] += 1
                            for k in range(KC):
                                K.op(PE, lambda e: e.matmul(yb[:, :], Sact[:, k, tl * 128:(tl + 1) * 128], w2[:, k, hf * 512:(hf + 1) * 512],
                                                            start=(k == 0), stop=False), reads=[Sact.b, w2.b], writes=[yb.b], inc=False)
                            K.op(PE, lambda e: e.matmul(yb[:, :], ones_bf[0:1, :], b2bf[0:1, hf * 512:(hf + 1) * 512], start=False, stop=True),
                                 reads=[b2bf.b, CONST], writes=[yb.b], inc=True)
                            residual(xg, tl, hf, yb, gtbc[mi][0], tmp[hf])
                            yield
                        store_tile(xg, lambda p0, n, tl=tl: xg[p0:p0 + n, tl, :], xdst(i, stile))

                load_u(0)
                if len(groups) > 1:
                    load_u(1)
                drain(A_gen(0))
                for gi in range(len(groups)):
                    if gi + 1 < len(groups):
                        interleave(A_gen(gi + 1), B_gen(gi), k=2)
                    else:
                        drain(B_gen(gi))
                    if gi + 2 < len(groups):
                        load_u(gi + 2)
                K.barrier()

    for i in range(nlayers):
        last = (i == DEPTH - 1)
        mod_phase(i)
        if dbg == "mod":
            K.dma(SP, out[0:128, 0:64], colv[:].rearrange("p v k m -> p (v k m)"), reads=[colv.b])
            K.dma(SP, out[128:256, :], gtbc[0][0][:], reads=[gtbc[0][0].b])
            K.dma(SP, out[256:384, :], gtbc[1][1][:], reads=[gtbc[1][1].b])
            K.dma(SP, out[384:512, 0:16], scT[:].rearrange("p k m -> p (k m)"), reads=[scT.b])
            K.dma(SP, out[512:640, 0:32], gmix[:].rearrange("p l k -> p (l k)"), reads=[gmix.b])
            break
        if i % 2 == 0:
            gla_phaseA(i)
            gla_phaseB(i)
        else:
            conv_phase(i, last)
        if dbg == "mix" and i == nlayers - 1:
            break
        ffn_phase(i, last)

    if dbg == "sptm":
        with ExitStack() as st:
            d_ = sb("dbgs", [128, 3072], stack=st)
            for tt in range(8):
                K.dma(SP, d_[:], SP_TM[tt * 128:(tt + 1) * 128, :], writes=[d_.b])
                for q in range(3):
                    K.dma(SP, out[tt * 384 + q * 128: tt * 384 + (q + 1) * 128, :], d_[:, q * 1024:(q + 1) * 1024], reads=[d_.b])
    elif dbg == "mod":
        pass
    elif nlayers < DEPTH or dbg == "mix":
        with ExitStack() as st:
            dt_ = [sb(f"dbg{q}", [128, 4, D], stack=st) for q in range(2)]
            for g in range(8):
                d_ = dt_[g % 2]
                for tl in range(4):
                    r = g * 512 + tl * 128
                    K.dma(SP, d_[:, tl, :], XD[r:r + 128, :], writes=[d_.b])
                for tl in range(4):
                    r = g * 512 + tl * 128
                    K.dma(SP, out[r:r + 128, :], d_[:, tl, :], reads=[d_.b])
    K.barrier()
    es.close()
    return nc, K


_CACHE = {}


def kernel(**inputs):
    nl = int(inputs.pop("_nlayers", DEPTH))
    if nl not in _CACHE:
        _CACHE[nl] = build(nl)[0]
    nc = _CACHE[nl]
    f32 = lambda a: np.ascontiguousarray(np.asarray(a, dtype=np.float32))
    shared = {}
    for k in ("w_mod", "b_mod", "norm_mix_g", "norm_ffn_g", "gla_w_in", "gla_wa_f", "gla_ba_f", "gla_wa_b", "gla_ba_b",
              "gla_norm_g", "gla_w_out", "conv_w_pw1", "conv_b_pw1", "conv_w_dw", "conv_b_dw", "conv_ln_g", "conv_ln_b",
              "conv_w_pw2", "conv_b_pw2", "ffn_w_in", "ffn_w_out"):
        shared[k] = f32(inputs[k])
    shared["final_norm_g"] = f32(inputs["final_norm_g"]).reshape(1, D)
    shared["c_ctx"] = f32(inputs["c_ctx"]).reshape(1, D)
    x = f32(inputs["x"])
    c = f32(inputs["c"])
    ctx = f32(inputs["ctx"])
    in_maps = []
    for b in range(8):
        m = dict(shared)
        m["x"] = x[b]
        m["ctx"] = ctx[b]
        m["c"] = c[b].reshape(1, D)
        in_maps.append(m)
    res = run_bass_kernel_spmd(nc, in_maps, core_ids=list(range(8)))
    return np.stack([np.asarray(r["out"], dtype=np.float32) for r in res.results], axis=0)
```
